# Optimizing a Trainium2 kernel written in Bass

```python
import jax, jax.numpy as jnp
from jax import lax
import numpy as np

D_MODEL = 1024
BATCH = 8
SEQ = 2048
DEPTH = 4

N_MIXERS = 3
MEM_LEN = 256
EPS = 1e-6
ROPE_THETA = 10000.0
MAX_POS_OFFSET = 4096

MLA_HEADS = 8
MLA_NOPE = 128
MLA_ROPE = 64
MLA_V = 128
MLA_Q_RANK = 384
MLA_KV_RANK = 256
Q_BLOCK = 128

GDN_HEADS = 8
GDN_DK = 128
GDN_DV = 128
GDN_CONV = 4
GDN_CHUNK = 64
GDN_QKV = GDN_HEADS * (2 * GDN_DK + GDN_DV)
GDN_PROJ = GDN_QKV + GDN_HEADS * GDN_DV + 2 * GDN_HEADS

SC_WIDTH = D_MODEL
SC_CONV = 3

X_HEADS = 4
X_HEAD_DIM = D_MODEL // X_HEADS

D_FF = 4 * D_MODEL

N_A = (DEPTH + 2) // N_MIXERS
N_B = (DEPTH + 1) // N_MIXERS
N_C = DEPTH // N_MIXERS

kernel_name = "hybrid_mla_gdn_shortconv_memxattn"


def rms_norm(x, g):
    xf = x.astype(jnp.float32)
    y = xf * lax.rsqrt(jnp.mean(xf * xf, axis=-1, keepdims=True) + EPS)
    return (y * g.astype(jnp.float32)).astype(x.dtype)


def rope_tables(positions):
    inv_freq = ROPE_THETA ** (-jnp.arange(0, MLA_ROPE, 2, dtype=jnp.float32) / MLA_ROPE)
    ang = positions.astype(jnp.float32)[..., None] * inv_freq
    return jnp.cos(ang), jnp.sin(ang)


def apply_rope(x, cos, sin):
    c = cos[:, :, None, :]
    s = sin[:, :, None, :]
    x1, x2 = jnp.split(x.astype(jnp.float32), 2, axis=-1)
    return jnp.concatenate([x1 * c - x2 * s, x2 * c + x1 * s], axis=-1).astype(x.dtype)


def causal_depthwise_conv(x, w):
    k, c = w.shape
    return lax.conv_general_dilated(
        x, w[:, None, :].astype(x.dtype), window_strides=(1,), padding=[(k - 1, 0)],
        dimension_numbers=("NWC", "WIO", "NWC"), feature_group_count=c)


def mla_mixer(h, cos, sin, w_in, q_norm, kv_norm, w_uq, w_ukv, w_o):
    b, s, _ = h.shape
    z = h @ w_in
    c_q, c_kv, k_rope = jnp.split(z, [MLA_Q_RANK, MLA_Q_RANK + MLA_KV_RANK], axis=-1)
    q = (rms_norm(c_q, q_norm) @ w_uq).reshape(b, s, MLA_HEADS, MLA_NOPE + MLA_ROPE)
    q_nope = q[..., :MLA_NOPE]
    q_rope = apply_rope(q[..., MLA_NOPE:], cos, sin)
    kv = (rms_norm(c_kv, kv_norm) @ w_ukv).reshape(b, s, MLA_HEADS, MLA_NOPE + MLA_V)
    k_nope, v = kv[..., :MLA_NOPE], kv[..., MLA_NOPE:]
    k_rope = apply_rope(k_rope[:, :, None, :], cos, sin)[:, :, 0, :]
    scale = (MLA_NOPE + MLA_ROPE) ** -0.5
    outs = []
    for start in range(0, s, Q_BLOCK):
        end = start + Q_BLOCK
        sc = (jnp.einsum("bqhd,bkhd->bhqk", q_nope[:, start:end], k_nope[:, :end])
              + jnp.einsum("bqhr,bkr->bhqk", q_rope[:, start:end], k_rope[:, :end]))
        sc = sc.astype(jnp.float32) * scale
        mask = (start + jnp.arange(Q_BLOCK))[:, None] >= jnp.arange(end)[None, :]
        sc = jnp.where(mask, sc, -jnp.inf)
        p = jax.nn.softmax(sc, axis=-1).astype(v.dtype)
        outs.append(jnp.einsum("bhqk,bkhd->bqhd", p, v[:, :end]))
    o = jnp.concatenate(outs, axis=1).reshape(b, s, MLA_HEADS * MLA_V)
    return o @ w_o


def chunk_gated_delta_rule(q, k, v, g, beta):
    b, s, h, dk = q.shape
    dv = v.shape[-1]
    c = GDN_CHUNK
    n = s // c

    def to_chunks(t):
        t = t.astype(jnp.float32).reshape((b, n, c, h) + t.shape[3:])
        return jnp.moveaxis(t, (1, 3), (0, 2))

    qc, kc, vc = to_chunks(q), to_chunks(k), to_chunks(v)
    gc = lax.cumsum(to_chunks(g), axis=3)
    bc = to_chunks(beta)
    tri = jnp.tril(jnp.ones((c, c), dtype=bool))
    strict = jnp.tril(jnp.ones((c, c), dtype=bool), -1)
    decay = jnp.exp(jnp.where(tri, gc[..., :, None] - gc[..., None, :], -jnp.inf))
    k_beta = kc * bc[..., None]
    m = jnp.where(strict, jnp.einsum("nbhid,nbhjd->nbhij", k_beta, kc) * decay, 0.0)
    eye = jnp.eye(c, dtype=jnp.float32)
    t_inv = lax.linalg.triangular_solve(eye + m, jnp.broadcast_to(eye, m.shape),
                                        left_side=True, lower=True, unit_diagonal=True)
    u = t_inv @ (vc * bc[..., None])
    w = t_inv @ (k_beta * jnp.exp(gc)[..., None])
    attn_intra = jnp.einsum("nbhid,nbhjd->nbhij", qc, kc) * decay

    def step(state, xs):
        q_i, k_i, u_i, w_i, g_i, a_i = xs
        v_new = u_i - w_i @ state
        o_i = (q_i * jnp.exp(g_i)[..., None]) @ state + a_i @ v_new
        g_last = g_i[..., -1:]
        state = (state * jnp.exp(g_last)[..., None]
                 + jnp.einsum("bhcd,bhce->bhde", k_i * jnp.exp(g_last - g_i)[..., None], v_new))
        return state, o_i

    s0 = jnp.zeros((b, h, dk, dv), jnp.float32)
    _, o = lax.scan(step, s0, (qc, kc, u, w, gc, attn_intra))
    return jnp.moveaxis(o, (0, 2), (1, 3)).reshape(b, s, h, dv)


def gdn_mixer(h, w_in, conv_w, a_log, dt_bias, o_norm, w_o):
    b, s, _ = h.shape
    z = h @ w_in
    qkv, gate, beta_logit, a_logit = jnp.split(
        z, [GDN_QKV, GDN_QKV + GDN_HEADS * GDN_DV, GDN_QKV + GDN_HEADS * GDN_DV + GDN_HEADS], axis=-1)
    qkv = jax.nn.silu(causal_depthwise_conv(qkv, conv_w))
    q, k, v = jnp.split(qkv, [GDN_HEADS * GDN_DK, 2 * GDN_HEADS * GDN_DK], axis=-1)
    q = q.reshape(b, s, GDN_HEADS, GDN_DK).astype(jnp.float32)
    k = k.reshape(b, s, GDN_HEADS, GDN_DK).astype(jnp.float32)
    v = v.reshape(b, s, GDN_HEADS, GDN_DV)
    q = q * lax.rsqrt(jnp.sum(q * q, -1, keepdims=True) + EPS) * (GDN_DK ** -0.5)
    k = k * lax.rsqrt(jnp.sum(k * k, -1, keepdims=True) + EPS)
    beta = jax.nn.sigmoid(beta_logit.astype(jnp.float32))
    g = -jnp.exp(a_log.astype(jnp.float32)) * jax.nn.softplus(
        a_logit.astype(jnp.float32) + dt_bias.astype(jnp.float32))
    o = chunk_gated_delta_rule(q, k, v, g, beta)
    o = rms_norm(o, o_norm) * jax.nn.silu(gate.reshape(b, s, GDN_HEADS, GDN_DV).astype(jnp.float32))
    return o.reshape(b, s, GDN_HEADS * GDN_DV).astype(h.dtype) @ w_o


def short_conv_mixer(h, w_in, conv_w, w_o):
    z = h @ w_in
    b_gate, c_gate, u = jnp.split(z, 3, axis=-1)
    y = b_gate * causal_depthwise_conv(c_gate * u, conv_w)
    return y @ w_o


def memory_cross_attention(h, mem_n, w_q, w_kv, w_o):
    b, s, _ = h.shape
    m = mem_n.shape[1]
    q = (h @ w_q).reshape(b, s, X_HEADS, X_HEAD_DIM)
    k, v = jnp.split(mem_n @ w_kv, 2, axis=-1)
    k = k.reshape(b, m, X_HEADS, X_HEAD_DIM)
    v = v.reshape(b, m, X_HEADS, X_HEAD_DIM)
    sc = jnp.einsum("bqhd,bkhd->bhqk", q, k).astype(jnp.float32) * (X_HEAD_DIM ** -0.5)
    p = jax.nn.softmax(sc, axis=-1).astype(v.dtype)
    o = jnp.einsum("bhqk,bkhd->bqhd", p, v).reshape(b, s, X_HEADS * X_HEAD_DIM)
    return o @ w_o


def relu2_mlp(h, w1, w2):
    return jnp.square(jax.nn.relu(h @ w1)) @ w2


def setup_inputs(seed: int = 0) -> dict:
    key = jax.random.key(seed)
    ks = iter(jax.random.split(key, 40))

    def w(shape, fan_in):
        return jax.random.normal(next(ks), shape, jnp.float32) * (fan_in ** -0.5)

    def gain(shape):
        return 1.0 + 0.02 * jax.random.normal(next(ks), shape, jnp.float32)

    x = jax.random.normal(next(ks), (BATCH, SEQ, D_MODEL), jnp.float32)
    mem = jax.random.normal(next(ks), (BATCH, MEM_LEN, D_MODEL), jnp.float32)
    offsets = jax.random.randint(next(ks), (BATCH, 1), 0, MAX_POS_OFFSET, dtype=jnp.int32)
    positions = offsets + jnp.arange(SEQ, dtype=jnp.int32)[None, :]

    mla_w_in = w((N_A, D_MODEL, MLA_Q_RANK + MLA_KV_RANK + MLA_ROPE), D_MODEL)
    mla_q_norm = gain((N_A, MLA_Q_RANK))
    mla_kv_norm = gain((N_A, MLA_KV_RANK))
    mla_w_uq = w((N_A, MLA_Q_RANK, MLA_HEADS * (MLA_NOPE + MLA_ROPE)), MLA_Q_RANK)
    mla_w_ukv = w((N_A, MLA_KV_RANK, MLA_HEADS * (MLA_NOPE + MLA_V)), MLA_KV_RANK)
    mla_w_o = w((N_A, MLA_HEADS * MLA_V, D_MODEL), MLA_HEADS * MLA_V)

    gdn_w_in = w((N_B, D_MODEL, GDN_PROJ), D_MODEL)
    gdn_conv_w = w((N_B, GDN_CONV, GDN_QKV), GDN_CONV)
    gdn_a_log = jnp.log(jax.random.uniform(next(ks), (N_B, GDN_HEADS), jnp.float32, 1.0, 16.0))
    dt = jnp.exp(jax.random.uniform(next(ks), (N_B, GDN_HEADS), jnp.float32,
                                    float(np.log(1e-3)), float(np.log(1e-1))))
    gdn_dt_bias = dt + jnp.log(-jnp.expm1(-dt))
    gdn_o_norm = gain((N_B, GDN_DV))
    gdn_w_o = w((N_B, GDN_HEADS * GDN_DV, D_MODEL), GDN_HEADS * GDN_DV)

    sc_w_in = w((N_C, D_MODEL, 3 * SC_WIDTH), D_MODEL)
    sc_conv_w = w((N_C, SC_CONV, SC_WIDTH), SC_CONV)
    sc_w_o = w((N_C, SC_WIDTH, D_MODEL), SC_WIDTH)

    norm_mix = gain((DEPTH, D_MODEL))
    norm_mem = gain((DEPTH, D_MODEL))
    norm_mlp = gain((DEPTH, D_MODEL))
    xa_w_q = w((DEPTH, D_MODEL, X_HEADS * X_HEAD_DIM), D_MODEL)
    xa_w_kv = w((DEPTH, D_MODEL, 2 * X_HEADS * X_HEAD_DIM), D_MODEL)
    xa_w_o = w((DEPTH, X_HEADS * X_HEAD_DIM, D_MODEL), X_HEADS * X_HEAD_DIM)
    mlp_w1 = w((DEPTH, D_MODEL, D_FF), D_MODEL)
    mlp_w2 = w((DEPTH, D_FF, D_MODEL), D_FF)
    mem_norm = gain((D_MODEL,))
    final_norm = gain((D_MODEL,))

    return {
        "x": x, "mem": mem, "positions": positions,
        "mla_w_in": mla_w_in, "mla_q_norm": mla_q_norm, "mla_kv_norm": mla_kv_norm,
        "mla_w_uq": mla_w_uq, "mla_w_ukv": mla_w_ukv, "mla_w_o": mla_w_o,
        "gdn_w_in": gdn_w_in, "gdn_conv_w": gdn_conv_w, "gdn_a_log": gdn_a_log,
        "gdn_dt_bias": gdn_dt_bias, "gdn_o_norm": gdn_o_norm, "gdn_w_o": gdn_w_o,
        "sc_w_in": sc_w_in, "sc_conv_w": sc_conv_w, "sc_w_o": sc_w_o,
        "norm_mix": norm_mix, "norm_mem": norm_mem, "norm_mlp": norm_mlp,
        "xa_w_q": xa_w_q, "xa_w_kv": xa_w_kv, "xa_w_o": xa_w_o,
        "mlp_w1": mlp_w1, "mlp_w2": mlp_w2,
        "mem_norm": mem_norm, "final_norm": final_norm,
    }


def reference(x, mem, positions,
              mla_w_in, mla_q_norm, mla_kv_norm, mla_w_uq, mla_w_ukv, mla_w_o,
              gdn_w_in, gdn_conv_w, gdn_a_log, gdn_dt_bias, gdn_o_norm, gdn_w_o,
              sc_w_in, sc_conv_w, sc_w_o,
              norm_mix, norm_mem, norm_mlp,
              xa_w_q, xa_w_kv, xa_w_o,
              mlp_w1, mlp_w2,
              mem_norm, final_norm):
    cos, sin = rope_tables(positions)
    mem_n = rms_norm(mem, mem_norm)
    for i in range(DEPTH):
        j = i // N_MIXERS
        kind = i % N_MIXERS
        h = rms_norm(x, norm_mix[i])
        if kind == 0:
            y = mla_mixer(h, cos, sin, mla_w_in[j], mla_q_norm[j], mla_kv_norm[j],
                          mla_w_uq[j], mla_w_ukv[j], mla_w_o[j])
        elif kind == 1:
            y = gdn_mixer(h, gdn_w_in[j], gdn_conv_w[j], gdn_a_log[j], gdn_dt_bias[j],
                          gdn_o_norm[j], gdn_w_o[j])
        else:
            y = short_conv_mixer(h, sc_w_in[j], sc_conv_w[j], sc_w_o[j])
        x = x + y
        x = x + memory_cross_attention(rms_norm(x, norm_mem[i]), mem_n,
                                       xa_w_q[i], xa_w_kv[i], xa_w_o[i])
        x = x + relu2_mlp(rms_norm(x, norm_mlp[i]), mlp_w1[i], mlp_w2[i])
    return rms_norm(x, final_norm)
```

```python
import numpy as np
import concourse.bass as bass
import concourse.mybir as mybir
from concourse.bass_utils import run_bass_kernel_spmd

F32 = mybir.dt.float32
BF16 = mybir.dt.bfloat16
I32 = mybir.dt.int32
AF = mybir.ActivationFunctionType
ALU = mybir.AluOpType

D = 1024
S = 2048
NT = 16
KC = 8
MEM = 256
EPS = 1e-6
SEM_EPOCH = 20000
BIG = 30000.0
SLOT_ELEMS = 4096
NSLOT = 3


class K:
    def __init__(self, nc):
        self.nc = nc
        self.E = {"pe": nc.tensor, "act": nc.scalar, "dve": nc.vector, "pool": nc.gpsimd, "sp": nc.sync}
        self.need = {e: set() for e in self.E}
        self.rank = {}
        self.esems = {e: [] for e in self.E}
        self.dma_sem_h = {}
        self.mode = "dry"
        self.reset()

    @property
    def dry(self):
        return self.mode == "dry"

    def reset(self):
        self.idx = {e: 0 for e in self.E}
        self.seen = {e: {} for e in self.E}
        self.last_w = {}
        self.readers = {}
        self.dma_cnt = {}
        self.rr = 0

    def finish_plan(self):
        for e in self.E:
            order = sorted(self.need[e])
            self.rank[e] = {}
            for r, i in enumerate(order):
                ep = r // SEM_EPOCH
                if ep >= len(self.esems[e]):
                    self.esems[e].append(self.nc.alloc_semaphore(f"s_{e}_{ep}"))
                self.rank[e][i] = (self.esems[e][ep], r % SEM_EPOCH + 1)

    def _collect(self, reads, writes):
        deps = {}

        def add(d):
            if d[0] not in deps or deps[d[0]][1] < d[1]:
                deps[d[0]] = d
        for k in reads:
            if k in self.last_w:
                add(self.last_w[k])
        for k in writes:
            if k in self.last_w:
                add(self.last_w[k])
            for d in self.readers.get(k, {}).values():
                add(d)
        return deps

    def _wait(self, eng, deps, skip_self=False):
        for name, (_, val) in deps.items():
            if skip_self and name == eng:
                continue
            if name.startswith("d_"):
                val = max(val, self.dma_cnt[name[2:]])
            if self.seen[eng].get(name, 0) >= val:
                continue
            self.seen[eng][name] = val
            if name.startswith("d_"):
                if self.mode == "emit":
                    self.E[eng].wait_ge(self.dma_sem_h[name[2:]], val)
            elif self.mode == "plan":
                self.need[name].add(val)
            else:
                sem, v = self.rank[name][val]
                self.E[eng].wait_ge(sem, v)

    def _record(self, d, reads, writes):
        for k in writes:
            self.last_w[k] = d
            self.readers[k] = {}
        for k in reads:
            if k in writes:
                continue
            self.readers.setdefault(k, {})[d[0]] = d

    def op(self, eng, fn, reads=(), writes=()):
        if self.mode == "dry":
            return None
        deps = self._collect(reads, writes)
        self._wait(eng, deps, skip_self=(eng == "pe"))
        self.idx[eng] += 1
        i = self.idx[eng]
        if self.mode == "emit":
            ins = fn()
            if i in self.rank[eng]:
                ins.then_inc(self.rank[eng][i][0], 1)
        self._record((eng, i), reads, writes)
        return None

    def dma(self, eng, out, in_, reads=(), writes=(), sname=None, **kw):
        if self.mode == "dry":
            return None
        serial = sname is None
        if serial:
            self.rr += 1
            sname = f"q_{eng}_{self.rr % 8}"
        if sname not in self.dma_sem_h:
            self.dma_sem_h[sname] = self.nc.alloc_semaphore(f"d_{sname}")
        self.dma_cnt.setdefault(sname, 0)
        deps = self._collect(reads, writes)
        if serial and self.dma_cnt[sname] > 0:
            deps[f"d_{sname}"] = (f"d_{sname}", self.dma_cnt[sname])
        self._wait(eng, deps)
        if self.mode == "emit":
            n0 = self.nc.n_instructions()
            ins = self.E[eng].dma_start(out=out, in_=in_, **kw)
            assert self.nc.n_instructions() - n0 == 1, "DMA was split"
            ins.then_inc(self.dma_sem_h[sname], 16)
        self.dma_cnt[sname] += 16
        self._record((f"d_{sname}", self.dma_cnt[sname]), reads, writes)
        return None

    def barrier(self):
        if self.mode == "dry":
            return
        alld = {}
        for e in self.E:
            if self.idx[e] > 0:
                alld[e] = (e, self.idx[e])
        for sname, val in self.dma_cnt.items():
            if val > 0:
                alld[f"d_{sname}"] = (f"d_{sname}", val)
        for e in self.E:
            self._wait(e, alld)
        self.last_w = {}
        self.readers = {}

    def final_wait(self, eng="sp"):
        alld = {}
        for sname, val in self.dma_cnt.items():
            if val > 0:
                alld[f"d_{sname}"] = (f"d_{sname}", val)
        self._wait(eng, alld)


def host_consts():
    c = {}
    p = np.arange(128)
    c["ident"] = np.eye(128, dtype=np.float32)
    c["ones"] = np.ones((128, 128), np.float32)
    c["maskC"] = (p[None, :] >= p[:, None]).astype(np.float32)
    same = (p[:, None] // 64) == (p[None, :] // 64)
    c["ltri"] = ((p[:, None] <= p[None, :]) & same).astype(np.float32)
    low = (p[None, :] <= p[:, None]) & same
    c["mposL"] = np.where(low, 0.0, BIG).astype(np.float32)
    c["mposU"] = np.where(low.T, 0.0, BIG).astype(np.float32)
    c["strict"] = ((p[None, :] < p[:, None]) & same).astype(np.float32)
    c["p2"] = ((p[:, None] % 64) == (p[None, :] % 64)).astype(np.float32)
    inv = (10000.0 ** (-np.arange(0, 64, 2, dtype=np.float32) / 64)).astype(np.float32)
    rc = np.zeros((128, 128), np.float32)
    rc[:, 0] = (inv[p % 32].astype(np.float64) / (2 * np.pi)).astype(np.float32)
    rc[:, 1] = np.where(p < 64, 0.25, np.where(p < 96, 0.5, 0.0))
    c["rc"] = rc
    names = ["ident", "ones", "maskC", "ltri", "mposL", "mposU", "strict", "p2", "rc"]
    return np.concatenate([c[n] for n in names], axis=1), names


W_NAMES = ["mla_w_in", "mla_q_norm", "mla_kv_norm", "mla_w_uq", "mla_w_ukv", "mla_w_o",
           "gdn_w_in", "gdn_conv_w", "gdn_a_log", "gdn_dt_bias", "gdn_o_norm", "gdn_w_o",
           "sc_w_in", "sc_conv_w", "sc_w_o", "norm_mix", "norm_mem", "norm_mlp",
           "xa_w_q", "xa_w_kv", "xa_w_o", "mlp_w1", "mlp_w2", "mem_norm", "final_norm"]
W_SHAPES = {
    "mla_w_in": [2, 1024, 704], "mla_q_norm": [2, 384], "mla_kv_norm": [2, 256],
    "mla_w_uq": [2, 384, 1536], "mla_w_ukv": [2, 256, 2048], "mla_w_o": [2, 1024, 1024],
    "gdn_w_in": [1, 1024, 4112], "gdn_conv_w": [1, 4, 3072], "gdn_a_log": [1, 8], "gdn_dt_bias": [1, 8],
    "gdn_o_norm": [1, 128], "gdn_w_o": [1, 1024, 1024],
    "sc_w_in": [1, 1024, 3072], "sc_conv_w": [1, 3, 1024], "sc_w_o": [1, 1024, 1024],
    "norm_mix": [4, 1024], "norm_mem": [4, 1024], "norm_mlp": [4, 1024],
    "xa_w_q": [4, 1024, 1024], "xa_w_kv": [4, 1024, 2048], "xa_w_o": [4, 1024, 1024],
    "mlp_w1": [4, 1024, 4096], "mlp_w2": [4, 4096, 1024], "mem_norm": [1024], "final_norm": [1024],
}


class Prog:
    def __init__(self, plan):
        self.plan = plan
        nc = bass.Bass("TRN2", target_bir_lowering=False)
        self.nc = nc
        self.k = K(nc)
        dt = nc.dram_tensor
        self.x_d = dt("x", [S, D], F32, kind="ExternalInput").ap()
        self.mem_d = dt("mem", [MEM, D], F32, kind="ExternalInput").ap()
        self.pos_d = dt("pos", [S], I32, kind="ExternalInput").ap()
        cst, names = host_consts()
        self.cst_d = dt("cst", list(cst.shape), F32, kind="ExternalInput").ap()
        self.cnames = names
        self.w = {n: dt(n, W_SHAPES[n], F32, kind="ExternalInput").ap() for n in W_NAMES}
        self.out_d = dt("out", [S, D], F32, kind="ExternalOutput").ap()
        self._alloc()
        self.pieces = []
        self.piece_ptr = 0
        self.issued = 0

    def _alloc(self):
        nc = self.nc
        a = nc.alloc_sbuf_tensor
        self.X = a("X", [128, NT, D], F32)
        self.HT = a("HT", [128, KC, S], BF16)
        self.W = [a(f"W{i}", [128, SLOT_ELEMS], BF16) for i in range(NSLOT)]
        self.ident_f = a("ident_f", [128, 128], F32)
        self.ident_b = a("ident_b", [128, 128], BF16)
        self.ones_b = a("ones_b", [128, 128], BF16)
        self.maskC = a("maskC", [128, 128], BF16)
        self.ltri = a("ltri", [128, 128], F32)
        self.mposL = a("mposL", [128, 128], F32)
        self.mposU = a("mposU", [128, 128], F32)
        self.strict = a("strict", [128, 128], F32)
        self.p2 = a("p2", [128, 128], BF16)
        self.rc = a("rc", [128, 128], F32)
        self.Gb = a("Gb", [128, D], F32)
        self.memnT = a("memnT", [128, KC, MEM], BF16)
        self.junk = a("junk", [128, D], BF16)
        self.hn = [a(f"hn{i}", [128, D], BF16) for i in range(2)]
        self.ss = a("ss", [128, 64], F32)
        self.small = a("small", [128, 256], F32)
        self.stage = a("stage", [128, 128], F32)
        self.stage2 = a("stage2", [128, 128], F32)
        self.SCRB = 55296 + 8192 + 5632
        self.scr = a("scr", [128, self.SCRB // 4], F32)
        self.ps = [nc.alloc_psum_tensor(f"ps{i}", [128, 512], F32) for i in range(8)]
        self.ps_i = 0
        print("sbuf bytes remaining", nc.sbuf_bytes_remaining)

    def sv(self, off, shape, dtype):
        esz = 4 if dtype in (F32, I32) else 2
        n = int(np.prod(shape))
        assert off % 4 == 0 and off + n * esz <= self.SCRB, (off, shape)
        ap = self.scr[:, off // 4: off // 4 + (n * esz) // 4]
        if dtype != F32:
            ap = ap.bitcast(dtype)
        if len(shape) == 2:
            ap = ap.rearrange("p (a b) -> p a b", a=shape[0], b=shape[1])
        elif len(shape) == 3:
            ap = ap.rearrange("p (a b c) -> p a b c", a=shape[0], b=shape[1], c=shape[2])
        return ap

    def bank(self, lo=None, hi=8):
        lo = getattr(self, "bank_lo", 0) if lo is None else lo
        i = lo + (self.ps_i % (hi - lo))
        self.ps_i += 1
        return self.ps[i], f"ps{i}"

    def wget(self, piece):
        k = self.k
        if k.dry:
            self.pieces.append(piece)
            return self.W[0], "W0"
        i = self.piece_ptr
        assert self.pieces[i] == piece, (self.pieces[i], piece)
        while self.issued < min(len(self.pieces), i + NSLOT - 1):
            self._load(self.issued)
            self.issued += 1
        self.piece_ptr += 1
        return self.W[i % NSLOT], f"W{i % NSLOT}"

    def _load(self, i):
        piece = self.pieces[i]
        slot = i % NSLOT
        Wt = self.W[slot]
        key = f"W{slot}"
        k = self.k
        w = self.w

        def ld(dst, src, **kw):
            k.dma("pool", dst, src, writes=[key], sname=f"w{slot}", **kw)

        def rows(ap2d):
            return ap2d.rearrange("(kc p) c -> p kc c", p=128)

        kind = piece[0]
        if kind == "cols":
            _, name, l, c0, ncol = piece
            src = w[name][l]
            nk = src.shape[0] // 128
            dst = Wt[:, 0:nk * ncol].rearrange("p (kc c) -> p kc c", kc=nk, c=ncol)
            ld(dst, rows(src[:, c0:c0 + ncol]))
        elif kind == "rows":
            _, name, l, r0, nr = piece
            src = w[name][l]
            nk = nr // 128
            ncol = src.shape[1]
            dst = Wt[:, 0:nk * ncol].rearrange("p (kc c) -> p kc c", kc=nk, c=ncol)
            ld(dst, rows(src[r0:r0 + nr, :]))
        elif kind == "multi":
            _, name, l, segs = piece
            src = w[name][l]
            nk = src.shape[0] // 128
            tot = sum(n for _, n in segs)
            dst = Wt[:, 0:nk * tot].rearrange("p (kc c) -> p kc c", kc=nk, c=tot)
            o = 0
            for c0, n in segs:
                ld(dst[:, :, o:o + n], rows(src[:, c0:c0 + n]))
                o += n
        elif kind == "mla_head":
            _, j, h = piece
            uq = w["mla_w_uq"][j]
            ukv = w["mla_w_ukv"][j]
            dq = Wt[:, 0:768].rearrange("p (kc c) -> p kc c", kc=3, c=256)
            b = h * 192
            ld(dq[:, :, 0:192], rows(uq[:, b:b + 192]))
            ld(dq[:, :, 192:224], rows(uq[:, b + 160:b + 192]))
            ld(dq[:, :, 224:256], rows(uq[:, b + 128:b + 160]))
            dkv = Wt[:, 768:1280].rearrange("p (kc c) -> p kc c", kc=2, c=256)
            ld(dkv, rows(ukv[:, h * 256:(h + 1) * 256]))
        else:
            raise ValueError(piece)

    def build(self):
        k = self.k
        k.mode = "dry"
        self._emit_all()
        for mode in ("plan", "emit"):
            k.mode = mode
            k.reset()
            self.ps_i = 0
            self.piece_ptr = 0
            self.issued = 0
            self._emit_all()
            k.final_wait("sp")
            if mode == "plan":
                k.finish_plan()
        return self.nc

    def _emit_all(self):
        self.emit_setup()
        for st in self.plan:
            getattr(self, "emit_" + st[0])(*st[1:])

    def emit_setup(self):
        k, nc = self.k, self.nc
        cd = self.cst_d
        ci = {n: i for i, n in enumerate(self.cnames)}

        def cload(eng, dst, name):
            i = ci[name]
            k.dma(eng, dst[:], cd[:, i * 128:(i + 1) * 128], writes=[dst.name])
        cload("sp", self.ident_f, "ident")
        cload("pool", self.ident_b, "ident")
        cload("pool", self.ones_b, "ones")
        cload("pool", self.maskC, "maskC")
        cload("sp", self.ltri, "ltri")
        cload("sp", self.mposL, "mposL")
        cload("sp", self.mposU, "mposU")
        cload("sp", self.strict, "strict")
        cload("pool", self.p2, "p2")
        cload("sp", self.rc, "rc")
        xv = self.x_d.rearrange("(t p) d -> p t d", p=128)
        for t in range(NT):
            k.dma("sp", self.X[:, t, :], xv[:, t, :], writes=[("X", t)])
        mm = self.sv(0, [2, D], F32)
        memv = self.mem_d.rearrange("(t p) d -> p t d", p=128)
        for t in range(2):
            k.dma("sp", mm[:, t, :], memv[:, t, :], writes=["mm"])
        self.norm_T(self.w["mem_norm"], lambda t: mm[:, t, :], lambda t: "mm", 2,
                    lambda t: self.memnT[:, :, t * 128:(t + 1) * 128], lambda t: "memnT")
        k.barrier()

    def load_cols(self, dst_ap, src2d, n, key):
        k, nc = self.k, self.nc
        k.dma("sp", self.stage[0:n, :], src2d, writes=["stage"])
        pb, pk = self.bank()
        k.op("pe", lambda: nc.tensor.transpose(pb[:, 0:n], self.stage[0:n, :], self.ident_f[0:n, 0:n]),
             reads=["stage", "ident_f"], writes=[pk])
        k.op("dve", lambda: nc.vector.tensor_copy(dst_ap, pb[:, 0:n]), writes=[pk, key])

    def norm_T(self, gain_ap, src, src_key, ntile, dst, dst_key):
        k, nc = self.k, self.nc
        ss = self.ss
        k.dma("sp", self.Gb[:], gain_ap.partition_broadcast(128), writes=["Gb"])
        k.op("dve", lambda: nc.vector.memset(ss[:, 0:16], 0.0), writes=["ss"])
        for t in range(ntile):
            k.op("act", lambda t=t: nc.scalar.activation(self.junk[:], src(t), AF.Square, accum_out=ss[:, t:t + 1]),
                 reads=[src_key(t)], writes=["junk", "ss"])
        k.op("act", lambda: nc.scalar.activation(ss[:, 16:16 + ntile], ss[:, 0:ntile], AF.Ln, scale=1.0 / D, bias=EPS),
             writes=["ss"])
        k.op("act", lambda: nc.scalar.activation(ss[:, 32:32 + ntile], ss[:, 16:16 + ntile], AF.Exp, scale=-0.5),
             writes=["ss"])
        for t in range(ntile):
            hb = self.hn[t % 2]
            hk = f"hn{t % 2}"
            k.op("dve", lambda t=t, hb=hb: nc.vector.scalar_tensor_tensor(hb[:], src(t), ss[:, 32 + t:33 + t], self.Gb[:],
                                                                           ALU.mult, ALU.mult),
                 reads=[src_key(t), "ss", "Gb"], writes=[hk])
            pb, pk = self.bank()
            pv = pb[:].bitcast(BF16).rearrange("p (a b) -> p a b", a=8, b=128)
            for c in range(KC):
                k.op("pe", lambda c=c, hb=hb, pv=pv: nc.tensor.transpose(pv[:, c, :], hb[:, c * 128:(c + 1) * 128], self.ident_b[:]),
                     reads=[hk, "ident_b"], writes=[pk])
            if t % 2 == 0:
                k.op("act", lambda t=t, pv=pv: nc.scalar.copy(dst(t), pv), writes=[pk, dst_key(t)])
            else:
                k.op("dve", lambda t=t, pv=pv: nc.vector.tensor_copy(dst(t), pv), writes=[pk, dst_key(t)])

    def norm_x(self, gain_ap):
        self.norm_T(gain_ap, lambda t: self.X[:, t, :], lambda t: ("X", t), NT,
                    lambda t: self.HT[:, :, t * 128:(t + 1) * 128], lambda t: ("HT", t))

    def htk(self, tb):
        return [("HT", 4 * tb + i) for i in range(4)]

    def add_to_x(self, pb, pk, tt, c0, n=512):
        k, nc = self.k, self.nc
        xs = self.X[:, tt, c0:c0 + n]
        k.op("dve", lambda: nc.vector.tensor_tensor(xs, pb[:, 0:n], xs, ALU.add), writes=[pk, ("X", tt)])

    def out_proj(self, src, src_key, name, l):
        k, nc = self.k, self.nc
        for j in range(2):
            Wt, wk = self.wget(("cols", name, l, j * 512, 512))
            Wv = Wt[:, 0:4096].rearrange("p (kc c) -> p kc c", kc=8, c=512)
            for tt in range(NT):
                pb, pk = self.bank()
                for c in range(KC):
                    k.op("pe", lambda c=c, pb=pb, tt=tt: nc.tensor.matmul(pb[:], src[:, c, tt * 128:(tt + 1) * 128], Wv[:, c, :],
                                                                         start=(c == 0), stop=(c == KC - 1)),
                         reads=[src_key(tt), wk], writes=[pk])
                self.add_to_x(pb, pk, tt, j * 512)

    def emit_mlp(self, l):
        k, nc = self.k, self.nc
        self.norm_x(self.w["norm_mlp"][l])
        k.barrier()
        H = [self.sv(i * 16384, [4, S], BF16) for i in range(2)]
        tmp = [self.sv(32768 + i * 1024, [1, 512], BF16) for i in range(2)]
        n = 0
        for g in range(8):
            W1, k1 = self.wget(("cols", "mlp_w1", l, g * 512, 512))
            W1v = W1[:, 0:4096].rearrange("p (kc c) -> p kc c", kc=8, c=512)
            W2, k2 = self.wget(("rows", "mlp_w2", l, g * 512, 512))
            W2v = W2[:, 0:4096].rearrange("p (kc c) -> p kc c", kc=4, c=1024)
            Hg = H[g % 2]
            hk = f"H{g % 2}"
            for fc in range(4):
                for tb in range(4):
                    pb, pk = self.bank()
                    for c in range(KC):
                        k.op("pe", lambda c=c, pb=pb, fc=fc, tb=tb: nc.tensor.matmul(
                            pb[:], W1v[:, c, fc * 128:(fc + 1) * 128], self.HT[:, c, tb * 512:(tb + 1) * 512],
                            start=(c == 0), stop=(c == KC - 1)), reads=[k1] + self.htk(tb), writes=[pk])
                    tm = tmp[n % 2]
                    tk = f"tmp{n % 2}"
                    n += 1
                    k.op("act", lambda pb=pb, tm=tm: nc.scalar.activation(tm[:, 0, :], pb[:], AF.Relu), writes=[pk, tk])
                    k.op("dve", lambda tm=tm, fc=fc, tb=tb, Hg=Hg: nc.vector.tensor_tensor(
                        Hg[:, fc, tb * 512:(tb + 1) * 512], tm[:, 0, :], tm[:, 0, :], ALU.mult),
                        reads=[tk], writes=[(hk, fc, tb)])
            for tt in range(NT):
                for db in range(2):
                    pb, pk = self.bank()
                    for fc in range(4):
                        k.op("pe", lambda fc=fc, pb=pb, tt=tt, db=db, Hg=Hg: nc.tensor.matmul(
                            pb[:], Hg[:, fc, tt * 128:(tt + 1) * 128], W2v[:, fc, db * 512:(db + 1) * 512],
                            start=(fc == 0), stop=(fc == 3)), reads=[k2, (hk, fc, tt // 4)], writes=[pk])
                    self.add_to_x(pb, pk, tt, db * 512)

    def emit_xattn(self, l):
        k, nc = self.k, self.nc
        self.norm_x(self.w["norm_mem"][l])
        k.barrier()
        QT = self.sv(0, [8, S], BF16)
        KT = self.sv(32768, [8, MEM], BF16)
        V = self.sv(36864, [2, D], BF16)
        PT = [self.sv(40960 + i * 1024, [1, 512], BF16) for i in range(4)]
        rec = self.sv(45056, [1, 512], F32)
        scale = 256 ** -0.5
        for j in range(4):
            Wt, wk = self.wget(("cols", "xa_w_kv", l, j * 512, 512))
            Wv = Wt[:, 0:4096].rearrange("p (kc c) -> p kc c", kc=8, c=512)
            if j < 2:
                for cc in range(4):
                    pb, pk = self.bank()
                    for c in range(KC):
                        k.op("pe", lambda c=c, pb=pb, cc=cc: nc.tensor.matmul(pb[:, 0:MEM], Wv[:, c, cc * 128:(cc + 1) * 128], self.memnT[:, c, :],
                                                                             start=(c == 0), stop=(c == KC - 1)),
                             reads=[wk, "memnT"], writes=[pk])
                    k.op("act", lambda pb=pb, j=j, cc=cc: nc.scalar.copy(KT[:, j * 4 + cc, :], pb[:, 0:MEM]), writes=[pk, "KT"])
            else:
                for kt in range(2):
                    pb, pk = self.bank()
                    for c in range(KC):
                        k.op("pe", lambda c=c, pb=pb, kt=kt: nc.tensor.matmul(pb[:], self.memnT[:, c, kt * 128:(kt + 1) * 128], Wv[:, c, :],
                                                                             start=(c == 0), stop=(c == KC - 1)),
                             reads=[wk, "memnT"], writes=[pk])
                    k.op("act", lambda pb=pb, j=j, kt=kt: nc.scalar.copy(V[:, kt, (j - 2) * 512:(j - 1) * 512], pb[:]), writes=[pk, "V"])
        n = 0
        for j in range(2):
            Wt, wk = self.wget(("cols", "xa_w_q", l, j * 512, 512))
            Wv = Wt[:, 0:4096].rearrange("p (kc c) -> p kc c", kc=8, c=512)
            for cc in range(4):
                for tb in range(4):
                    pb, pk = self.bank()
                    for c in range(KC):
                        k.op("pe", lambda c=c, pb=pb, cc=cc, tb=tb: nc.tensor.matmul(
                            pb[:], Wv[:, c, cc * 128:(cc + 1) * 128], self.HT[:, c, tb * 512:(tb + 1) * 512],
                            start=(c == 0), stop=(c == KC - 1)), reads=[wk] + self.htk(tb), writes=[pk])
                    dst = QT[:, j * 4 + cc, tb * 512:(tb + 1) * 512]
                    if n % 2 == 0:
                        k.op("act", lambda pb=pb, dst=dst: nc.scalar.copy(dst, pb[:]), writes=[pk, ("QT", j * 4 + cc, tb)])
                    else:
                        k.op("dve", lambda pb=pb, dst=dst: nc.vector.tensor_copy(dst, pb[:]), writes=[pk, ("QT", j * 4 + cc, tb)])
                    n += 1
        n = 0
        for h in range(4):
            for tb in range(4):
                pts = []
                for kt in range(2):
                    pb, pk = self.bank()
                    for c in range(2):
                        k.op("pe", lambda c=c, pb=pb, kt=kt: nc.tensor.matmul(
                            pb[:], KT[:, h * 2 + c, kt * 128:(kt + 1) * 128], QT[:, h * 2 + c, tb * 512:(tb + 1) * 512],
                            start=(c == 0), stop=(c == 1)), reads=["KT", ("QT", h * 2 + c, tb)], writes=[pk])
                    pt = PT[n % 4]
                    ptk = f"PT{n % 4}"
                    n += 1
                    k.op("act", lambda pb=pb, pt=pt: nc.scalar.activation(pt[:, 0, :], pb[:], AF.Exp, scale=scale), writes=[pk, ptk])
                    pts.append((pt, ptk))
                pb, pk = self.bank()
                for kt in range(2):
                    k.op("pe", lambda kt=kt, pb=pb: nc.tensor.matmul(pb[:], self.ones_b[:], pts[kt][0][:, 0, :], start=(kt == 0), stop=(kt == 1)),
                         reads=["ones_b", pts[kt][1]], writes=[pk])
                k.op("act", lambda pb=pb: nc.scalar.activation(rec[:, 0, :], pb[:], AF.Ln), writes=[pk, "rec"])
                k.op("act", lambda: nc.scalar.activation(rec[:, 0, :], rec[:, 0, :], AF.Exp, scale=-1.0), writes=["rec"])
                for c in range(2):
                    pb, pk = self.bank()
                    for kt in range(2):
                        k.op("pe", lambda kt=kt, pb=pb, c=c: nc.tensor.matmul(
                            pb[:], V[:, kt, h * 256 + c * 128:h * 256 + (c + 1) * 128], pts[kt][0][:, 0, :],
                            start=(kt == 0), stop=(kt == 1)), reads=["V", pts[kt][1]], writes=[pk])
                    k.op("dve", lambda pb=pb, c=c: nc.vector.tensor_tensor(
                        self.HT[:, h * 2 + c, tb * 512:(tb + 1) * 512], pb[:], rec[:, 0, :], ALU.mult),
                        reads=["rec"], writes=[pk] + self.htk(tb))
        self.out_proj(self.HT, lambda tt: ("HT", tt), "xa_w_o", l)

    def emit_sc(self, l, j):
        k, nc = self.k, self.nc
        self.norm_x(self.w["norm_mix"][l])
        k.barrier()
        YT = self.sv(0, [8, S], BF16)
        uv = self.sv(32768, [1, S], F32)
        cu = self.sv(40960, [1, S + 2], F32)
        bb = self.sv(49168, [1, S], BF16)
        cw = self.small[:, 0:24].rearrange("p (j c) -> p c j", c=8, j=3)
        self.load_cols(self.small[:, 0:24], self.w["sc_conv_w"][j].rearrange("j (c p) -> (j c) p", p=128), 24, "cw")
        k.op("dve", lambda: nc.vector.memset(cu[:, 0, 0:2], 0.0), writes=["cu"])
        for cc in range(8):
            Wt, wk = self.wget(("multi", "sc_w_in", j, ((cc * 128, 128), (1024 + cc * 128, 128), (2048 + cc * 128, 128))))
            Wv = Wt[:, 0:3072].rearrange("p (kc c) -> p kc c", kc=8, c=384)
            for part in (2, 1, 0):
                for tb in range(4):
                    pb, pk = self.bank()
                    for c in range(KC):
                        k.op("pe", lambda c=c, pb=pb, tb=tb, part=part: nc.tensor.matmul(
                            pb[:], Wv[:, c, part * 128:(part + 1) * 128], self.HT[:, c, tb * 512:(tb + 1) * 512],
                            start=(c == 0), stop=(c == KC - 1)), reads=[wk] + self.htk(tb), writes=[pk])
                    sl = slice(tb * 512, (tb + 1) * 512)
                    if part == 2:
                        k.op("act", lambda pb=pb, sl=sl: nc.scalar.copy(uv[:, 0, sl], pb[:]), writes=[pk, "uv"])
                    elif part == 1:
                        k.op("dve", lambda pb=pb, tb=tb, sl=sl: nc.vector.tensor_tensor(
                            cu[:, 0, 2 + tb * 512:2 + (tb + 1) * 512], pb[:], uv[:, 0, sl], ALU.mult), reads=["uv"], writes=[pk, "cu"])
                    else:
                        k.op("act", lambda pb=pb, sl=sl: nc.scalar.copy(bb[:, 0, sl], pb[:]), writes=[pk, "bb"])
            k.op("dve", lambda cc=cc: nc.vector.tensor_scalar(uv[:, 0, :], cu[:, 0, 2:S + 2], cw[:, cc, 2:3], None, ALU.mult),
                 reads=["cu", "cw"], writes=["uv"])
            k.op("dve", lambda cc=cc: nc.vector.scalar_tensor_tensor(uv[:, 0, :], cu[:, 0, 1:S + 1], cw[:, cc, 1:2], uv[:, 0, :], ALU.mult, ALU.add),
                 reads=["cu", "cw"], writes=["uv"])
            k.op("dve", lambda cc=cc: nc.vector.scalar_tensor_tensor(uv[:, 0, :], cu[:, 0, 0:S], cw[:, cc, 0:1], uv[:, 0, :], ALU.mult, ALU.add),
                 reads=["cu", "cw"], writes=["uv"])
            k.op("dve", lambda cc=cc: nc.vector.tensor_tensor(YT[:, cc, :], uv[:, 0, :], bb[:, 0, :], ALU.mult),
                 reads=["uv", "bb"], writes=[("YT", cc)])
        self.out_proj_generic(YT, "sc_w_o", j)

    def out_proj_generic(self, src, name, l):
        k, nc = self.k, self.nc
        for j in range(2):
            Wt, wk = self.wget(("cols", name, l, j * 512, 512))
            Wv = Wt[:, 0:4096].rearrange("p (kc c) -> p kc c", kc=8, c=512)
            for tt in range(NT):
                pb, pk = self.bank()
                for c in range(KC):
                    k.op("pe", lambda c=c, pb=pb, tt=tt: nc.tensor.matmul(pb[:], src[:, c, tt * 128:(tt + 1) * 128], Wv[:, c, :],
                                                                         start=(c == 0), stop=(c == KC - 1)),
                         reads=[("YT", c), wk], writes=[pk])
                self.add_to_x(pb, pk, tt, j * 512)

    def emit_cs(self):
        k, nc = self.k, self.nc
        self.CS = self.sv(55296, [S], F32)
        pos_i = self.sv(0, [S], I32)
        tf = self.sv(8192, [S], F32)
        ti = self.sv(16384, [S], I32)
        tg = self.sv(24576, [S], F32)
        k.dma("sp", pos_i, self.pos_d.partition_broadcast(128), writes=["pos_i"])
        k.op("dve", lambda: nc.vector.tensor_copy(tf, pos_i), reads=["pos_i"], writes=["tf"])
        k.op("dve", lambda: nc.vector.tensor_scalar(tf, tf, self.rc[:, 0:1], self.rc[:, 1:2], ALU.mult, ALU.add),
             reads=["rc"], writes=["tf"])
        k.op("dve", lambda: nc.vector.tensor_copy(ti, tf), reads=["tf"], writes=["ti"])
        k.op("dve", lambda: nc.vector.tensor_copy(tg, ti), reads=["ti"], writes=["tg"])
        k.op("dve", lambda: nc.vector.tensor_tensor(tf, tf, tg, ALU.subtract), reads=["tg"], writes=["tf"])
        k.op("act", lambda: nc.scalar.activation(self.CS, tf, AF.Sin, scale=float(2 * np.pi)), reads=["tf"], writes=["CS"])
        k.barrier()

    def emit_mla(self, l, j):
        k, nc = self.k, self.nc
        self.norm_x(self.w["norm_mix"][l])
        k.barrier()
        self.emit_cs()
        cqnT = self.sv(0, [3, S], BF16)
        ckvnT = self.sv(12288, [2, S], BF16)
        KR2 = self.sv(20480, [1, S], BF16)
        QN = self.sv(24576, [1, S], BF16)
        QR = self.sv(28672, [1, S], BF16)
        KN = self.sv(32768, [1, S], BF16)
        Vh = self.sv(36864, [NT, 128], BF16)
        PT = [self.sv(40960 + i * 1024, [1, 512], BF16) for i in range(2)]
        rec = self.sv(43008, [1, 512], F32)
        zc = [self.sv(45056 + i * 2048, [1, 512], F32) for i in range(2)]
        sq = self.sv(49152, [5, 512], BF16)
        tk = self.sv(54272, [1, 512], BF16)
        gq = self.small[:, 32:35]
        gkv = self.small[:, 36:38]
        self.load_cols(gq, self.w["mla_q_norm"][j].rearrange("(f p) -> f p", p=128), 3, "gq")
        self.load_cols(gkv, self.w["mla_kv_norm"][j].rearrange("(f p) -> f p", p=128), 2, "gq")
        WA, ka = self.wget(("cols", "mla_w_in", j, 0, 512))
        WAv = WA[:, 0:4096].rearrange("p (kc c) -> p kc c", kc=8, c=512)
        WB, kb = self.wget(("multi", "mla_w_in", j, ((512, 192), (672, 32), (640, 32))))
        WBv = WB[:, 0:2048].rearrange("p (kc c) -> p kc c", kc=8, c=256)
        scale = 192 ** -0.5
        nz = 0
        for tb in range(4):
            sl = slice(tb * 512, (tb + 1) * 512)
            zlist = []
            for f in range(6):
                pb, pk = self.ps[f], f"ps{f}"
                if f < 4:
                    lw = lambda c, f=f: WAv[:, c, f * 128:(f + 1) * 128]
                    wk = ka
                else:
                    lw = lambda c, f=f: WBv[:, c, (f - 4) * 128:(f - 3) * 128]
                    wk = kb
                for c in range(KC):
                    k.op("pe", lambda c=c, pb=pb, lw=lw: nc.tensor.matmul(pb[:], lw(c), self.HT[:, c, sl], start=(c == 0), stop=(c == KC - 1)),
                         reads=[wk] + self.htk(tb), writes=[pk])
                if f < 5:
                    k.op("act", lambda pb=pb, f=f: nc.scalar.activation(sq[:, f, :], pb[:], AF.Square), writes=[pk, ("sq", f)])
                    zlist.append((pb, pk))
                else:
                    k.op("dve", lambda pb=pb: nc.vector.tensor_tensor(tk[:, 0, :], pb[:], self.CS[:, sl], ALU.mult), reads=["CS"], writes=[pk, "tk"])
                    pb2, pk2 = self.ps[6], "ps6"
                    k.op("pe", lambda pb2=pb2: nc.tensor.matmul(pb2[:], self.p2[:], tk[:, 0, :], start=True, stop=True), reads=["p2", "tk"], writes=[pk2])
                    k.op("act", lambda pb2=pb2: nc.scalar.copy(KR2[:, 0, sl], pb2[:]), writes=[pk2, "KR2"])
            for (f0, f1, nf, gg, dstT, dk) in ((0, 3, 384, gq, cqnT, "cqnT"), (3, 5, 256, gkv, ckvnT, "ckvnT")):
                pb, pk = self.ps[7], "ps7"
                for f in range(f0, f1):
                    k.op("pe", lambda f=f, pb=pb: nc.tensor.matmul(pb[:], self.ones_b[:], sq[:, f, :], start=(f == f0), stop=(f == f1 - 1)),
                         reads=["ones_b", ("sq", f)], writes=[pk])
                k.op("act", lambda pb=pb: nc.scalar.activation(rec[:, 0, :], pb[:], AF.Ln, scale=1.0 / nf, bias=EPS), writes=[pk, "rec"])
                k.op("act", lambda: nc.scalar.activation(rec[:, 0, :], rec[:, 0, :], AF.Exp, scale=-0.5), writes=["rec"])
                for f in range(f0, f1):
                    zb, zk = zlist[f]
                    z = zc[nz % 2]
                    zkk = f"zc{nz % 2}"
                    nz += 1
                    k.op("act", lambda zb=zb, z=z: nc.scalar.copy(z[:, 0, :], zb[:]), writes=[zk, zkk])
                    k.op("dve", lambda z=z, f=f, f0=f0, gg=gg, dstT=dstT: nc.vector.scalar_tensor_tensor(
                        dstT[:, f - f0, sl], z[:, 0, :], gg[:, f - f0:f - f0 + 1], rec[:, 0, :], ALU.mult, ALU.mult),
                        reads=[zkk, "gq", "rec"], writes=[(dk, tb)])
        for h in range(8):
            Wh, wk = self.wget(("mla_head", j, h))
            Wq = Wh[:, 0:768].rearrange("p (kc c) -> p kc c", kc=3, c=256)
            Wkv = Wh[:, 768:1280].rearrange("p (kc c) -> p kc c", kc=2, c=256)
            for tb in range(4):
                sl = slice(tb * 512, (tb + 1) * 512)
                pb, pk = self.bank(4, 8)
                for c in range(3):
                    k.op("pe", lambda c=c, pb=pb: nc.tensor.matmul(pb[:], Wq[:, c, 0:128], cqnT[:, c, sl], start=(c == 0), stop=(c == 2)),
                         reads=[wk, ("cqnT", tb)], writes=[pk])
                k.op("act", lambda pb=pb: nc.scalar.copy(QN[:, 0, sl], pb[:]), writes=[pk, ("QN", tb)])
                pb, pk = self.bank(4, 8)
                for c in range(3):
                    k.op("pe", lambda c=c, pb=pb: nc.tensor.matmul(pb[:], Wq[:, c, 128:256], cqnT[:, c, sl], start=(c == 0), stop=(c == 2)),
                         reads=[wk, ("cqnT", tb)], writes=[pk])
                k.op("dve", lambda pb=pb: nc.vector.tensor_tensor(QR[:, 0, sl], pb[:], self.CS[:, sl], ALU.mult), reads=["CS"], writes=[pk, ("QR", tb)])
                pb, pk = self.bank(4, 8)
                for c in range(2):
                    k.op("pe", lambda c=c, pb=pb: nc.tensor.matmul(pb[:], Wkv[:, c, 0:128], ckvnT[:, c, sl], start=(c == 0), stop=(c == 1)),
                         reads=[wk, ("ckvnT", tb)], writes=[pk])
                k.op("act", lambda pb=pb: nc.scalar.copy(KN[:, 0, sl], pb[:]), writes=[pk, ("KN", tb)])
            for g4 in range(4):
                pb, pk = self.bank(4, 8)
                for i in range(4):
                    kt = g4 * 4 + i
                    for c in range(2):
                        k.op("pe", lambda c=c, pb=pb, kt=kt, i=i: nc.tensor.matmul(
                            pb[:, i * 128:(i + 1) * 128], ckvnT[:, c, kt * 128:(kt + 1) * 128], Wkv[:, c, 128:256],
                            start=(c == 0), stop=(c == 1)), reads=[wk, ("ckvnT", g4)], writes=[pk])
                k.op("dve", lambda pb=pb, g4=g4: nc.vector.tensor_copy(
                    Vh[:, g4 * 4:(g4 + 1) * 4, :], pb[:].rearrange("p (a b) -> p a b", a=4, b=128)), writes=[pk, ("Vh", g4)])
            nb = 0
            for qb in range(4):
                nkt = 4 * qb + 4
                b0 = 4 * (qb % 2)
                po, pok = self.ps[b0], f"ps{b0}"
                pd, pdk = self.ps[b0 + 1], f"ps{b0 + 1}"

                def score(kt):
                    q0 = max(qb * 512, kt * 128)
                    off = q0 - qb * 512
                    pb, pk = (self.ps[2], "ps2") if kt % 2 == 0 else (self.ps[3], "ps3")
                    k.op("pe", lambda: nc.tensor.matmul(pb[:, off:512], KN[:, 0, kt * 128:(kt + 1) * 128], QN[:, 0, q0:(qb + 1) * 512],
                                                        start=True, stop=False), reads=[("KN", kt // 4), ("QN", qb)], writes=[pk])
                    k.op("pe", lambda: nc.tensor.matmul(pb[:, off:512], KR2[:, 0, kt * 128:(kt + 1) * 128], QR[:, 0, q0:(qb + 1) * 512],
                                                        start=False, stop=True), reads=["KR2", ("QR", qb)], writes=[pk])
                    pt = PT[kt % 2]
                    ptk = f"PT{kt % 2}"
                    k.op("act", lambda: nc.scalar.activation(pt[:, 0, off:512], pb[:, off:512], AF.Exp, scale=scale), writes=[pk, ptk])
                    if kt >= 4 * qb:
                        k.op("dve", lambda: nc.vector.tensor_tensor(pt[:, 0, off:off + 128], pt[:, 0, off:off + 128], self.maskC[:], ALU.mult),
                             reads=["maskC"], writes=[ptk])
                    return pt, ptk, off

                def pv(kt, st):
                    pt, ptk, off = st
                    k.op("pe", lambda: nc.tensor.matmul(po[:, off:512], Vh[:, kt, :], pt[:, 0, off:512], start=(kt == 0), stop=(kt == nkt - 1)),
                         reads=[("Vh", kt // 4), ptk], writes=[pok])
                    k.op("pe", lambda: nc.tensor.matmul(pd[:, off:512], self.ones_b[:], pt[:, 0, off:512], start=(kt == 0), stop=(kt == nkt - 1)),
                         reads=["ones_b", ptk], writes=[pdk])
                prev = score(0)
                for kt in range(nkt):
                    nxt = score(kt + 1) if kt + 1 < nkt else None
                    pv(kt, prev)
                    prev = nxt
                k.op("act", lambda: nc.scalar.activation(rec[:, 0, :], pd[:], AF.Ln), writes=[pdk, "rec"])
                k.op("act", lambda: nc.scalar.activation(rec[:, 0, :], rec[:, 0, :], AF.Exp, scale=-1.0), writes=["rec"])
                k.op("dve", lambda: nc.vector.tensor_tensor(self.HT[:, h, qb * 512:(qb + 1) * 512], po[:], rec[:, 0, :], ALU.mult),
                     reads=["rec"], writes=[pok] + self.htk(qb))
        self.out_proj(self.HT, lambda tt: ("HT", tt), "mla_w_o", j)

    def emit_gdn(self, l, j):
        k, nc = self.k, self.nc
        self.norm_x(self.w["norm_mix"][l])
        k.barrier()
        off = [0]

        def A(shape, dtype, at=None):
            esz = 4 if dtype in (F32, I32) else 2
            n = (int(np.prod(shape)) * esz + 3) // 4 * 4
            if at is not None:
                return self.sv(at, shape, dtype)
            ap = self.sv(off[0], shape, dtype)
            off[0] += n
            return ap
        H = 8
        NCH = 32
        T = {n: A([NCH, H], F32)[0:64] for n in ("beta", "g", "gc", "kfac", "bg")}
        convw = A([96], F32)
        onb = A([128], F32)
        carry = A([3, 3], F32)
        qT = A([512], BF16)
        kT = A([512], BF16)
        vT = A([512], BF16)
        gT = A([512], BF16)
        qeT = A([512], BF16)
        kbg = A([8, 128], BF16)[0:64]
        kdec2 = [A([8, 128], BF16)[0:64] for _ in range(2)]
        vb = A([8, 128], BF16)[0:64]
        sg2 = [A([8, 128], BF16)[0:64] for _ in range(2)]
        o64 = A([8, 128], F32)[0:64]
        mat_off = off[0]
        mt = {n: A([8, 64], F32)[0:64] for n in ("T1", "Dec", "DecT", "M", "Mt", "A1", "At1", "Pt")}
        for i, n in enumerate(("dtb", "alog", "ta", "tb")):
            T[n] = A([NCH, H], F32, at=mat_off + i * 1024)[0:64]
        Ttb = A([8, 64], BF16)[0:64]
        attnT = A([8, 64], BF16)[0:64]
        wTb = A([8, 64], BF16)
        u_off = off[0]
        u_sb = A([8, 128], F32)[0:64]
        zin = [A([515], F32), A([515], F32), A([515], F32, at=mat_off)]
        zink = ["zin0", "zin1", "zin2"]
        acc = [A([512], F32) for _ in range(3)]
        og = acc[1].bitcast(BF16)[0:64, :].rearrange("p (a b) -> p a b", a=8, b=128)
        ogT = acc[2].bitcast(BF16)[:, 0:512]
        sqo = acc[0].bitcast(BF16)[0:64, :].rearrange("p (a b) -> p a b", a=8, b=128)
        sq = [A([512], BF16) for _ in range(2)]
        rs = [A([512], F32) for _ in range(2)]
        vnb = A([128], BF16)[0:64]
        Sf = A([128], F32)
        Sb = A([128], BF16)
        egl8 = A([8], F32)
        ssg = A([32], F32)[0:64]
        ones_f = self.stage2
        ltri64 = self.ltri[0:64, 0:64]
        idf = self.ident_f[0:64, 0:64]
        idb64 = self.ident_b[0:64, 0:64]
        B3 = [64, 8, 64]

        def bc(ap2d):
            return ap2d.unsqueeze(1).to_broadcast(B3)

        def dram_bcast(ap1d, n0, n1):
            return bass.AP(ap1d.tensor, ap1d.offset, [[0, 64], [0, n1], [1, n0]])
        k.dma("sp", T["dtb"], dram_bcast(self.w["gdn_dt_bias"][j], H, NCH), writes=["dtb"])
        k.dma("sp", T["alog"], dram_bcast(self.w["gdn_a_log"][j], H, NCH), writes=["alog"])
        k.dma("sp", onb, self.w["gdn_o_norm"][j].partition_broadcast(128), writes=["onb"])
        k.op("dve", lambda: nc.vector.memset(ones_f[:], 1.0), writes=["ones_f"])
        self.load_cols(convw, self.w["gdn_conv_w"][j].rearrange("j (c p) -> (j c) p", p=128), 96, "convw")
        Wt, wk = self.wget(("cols", "gdn_w_in", j, 4096, 16))
        Wv = Wt[:, 0:128].rearrange("p (kc c) -> p kc c", kc=8, c=16)
        pb, pk = self.bank()
        for c in range(NCH):
            for kc in range(KC):
                k.op("pe", lambda c=c, kc=kc: nc.tensor.matmul(pb[0:64, c * 16:(c + 1) * 16], self.HT[:, kc, c * 64:(c + 1) * 64], Wv[:, kc, :],
                                                                start=(kc == 0), stop=(kc == KC - 1)),
                     reads=[wk, ("HT", c // 2)], writes=[pk])
        pbv = pb[0:64, :].rearrange("p (c f) -> p c f", c=NCH, f=16)
        k.op("act", lambda: nc.scalar.activation(T["ta"], pbv[:, :, 0:8], AF.Exp, scale=-1.0), writes=[pk, "ta"])
        k.op("dve", lambda: nc.vector.tensor_scalar(T["ta"], T["ta"], 1.0, None, ALU.add), writes=["ta"])
        k.op("dve", lambda: nc.vector.reciprocal(T["beta"], T["ta"]), reads=["ta"], writes=["beta"])
        k.op("dve", lambda: nc.vector.tensor_tensor(T["tb"], pbv[:, :, 8:16], T["dtb"], ALU.add), reads=["dtb"], writes=[pk, "tb"])
        k.op("act", lambda: nc.scalar.activation(T["tb"], T["tb"], AF.Exp), writes=["tb"])
        k.op("act", lambda: nc.scalar.activation(T["tb"], T["tb"], AF.Ln, bias=1.0), writes=["tb"])
        k.op("act", lambda: nc.scalar.activation(T["alog"], T["alog"], AF.Exp), writes=["alog"])
        k.op("dve", lambda: nc.vector.scalar_tensor_tensor(T["g"], T["tb"], -1.0, T["alog"], ALU.mult, ALU.mult), reads=["tb", "alog"], writes=["g"])
        pb, pk = self.bank()
        pb2, pk2 = self.bank()
        for c in range(NCH):
            k.op("pe", lambda c=c: nc.tensor.matmul(pb[0:64, c * 8:(c + 1) * 8], ltri64, T["g"][:, c, :], start=True, stop=True),
                 reads=["g", "ltri"], writes=[pk])
            k.op("pe", lambda c=c: nc.tensor.matmul(pb2[0:64, c * 8:(c + 1) * 8], ones_f[0:64, 0:64], T["g"][:, c, :], start=True, stop=True),
                 reads=["g", "ones_f"], writes=[pk2])
        k.op("dve", lambda: nc.vector.tensor_copy(T["gc"], pb[0:64, 0:256].rearrange("p (c f) -> p c f", c=NCH, f=8)), writes=[pk, "gc"])
        k.op("dve", lambda: nc.vector.tensor_tensor(T["ta"], pb2[0:64, 0:256].rearrange("p (c f) -> p c f", c=NCH, f=8), T["gc"], ALU.subtract),
             reads=["gc"], writes=[pk2, "ta"])
        k.op("act", lambda: nc.scalar.activation(T["kfac"], T["ta"], AF.Exp), reads=["ta"], writes=["kfac"])
        k.op("act", lambda: nc.scalar.activation(T["tb"], T["gc"], AF.Exp), reads=["gc"], writes=["tb"])
        k.op("dve", lambda: nc.vector.tensor_tensor(T["bg"], T["beta"], T["tb"], ALU.mult), reads=["beta", "tb"], writes=["bg"])
        k.barrier()
        n_alt = [0]

        def cp(dst, src, reads=(), writes=()):
            n_alt[0] += 1
            if n_alt[0] % 2 == 0:
                k.op("act", lambda: nc.scalar.copy(dst, src), reads=reads, writes=writes)
            else:
                k.op("dve", lambda: nc.vector.tensor_copy(dst, src), reads=reads, writes=writes)

        def v3(pbank, parts=64):
            return pbank[0:parts, :].rearrange("p (c j) -> p c j", c=8, j=64)

        units = [(h, seg) for h in range(H) for seg in range(4)]
        B4 = [64, 8, 128]
        wts = {}

        def get_w(h):
            if h not in wts:
                Wh, wkh = self.wget(("multi", "gdn_w_in", j, ((h * 128, 128), (1024 + h * 128, 128), (2048 + h * 128, 128), (3072 + h * 128, 128))))
                wts[h] = (Wh[:, 0:4096].rearrange("p (kc c) -> p kc c", kc=8, c=512), wkh)
            return wts[h]

        def get_wo(h):
            if ("o", h) not in wts:
                wts[("o", h)] = self.wget(("rows", "gdn_w_o", j, h * 128, 128))
            return wts[("o", h)]

        def P(u):
            h, seg = units[u]
            Whv, wkh = get_w(h)
            t0 = seg * 512
            c0 = seg * 8
            hts = self.htk(seg)
            if seg == 0:
                k.op("dve", lambda: nc.vector.memset(carry, 0.0), writes=["carry"])
            banks = {}
            for part in (0, 1, 2, 3):
                pb, pk = (self.ps[0], "ps0") if part == 3 else (self.ps[1 + part], f"ps{1 + part}")
                banks[part] = (pb, pk)
                for kc in range(KC):
                    k.op("pe", lambda kc=kc: nc.tensor.matmul(pb[:], Whv[:, kc, part * 128:(part + 1) * 128], self.HT[:, kc, t0:t0 + 512],
                                                              start=(kc == 0), stop=(kc == KC - 1)), reads=[wkh] + hts, writes=[pk])
                    if kc % 4 == 3 and part < 3:
                        yield
            for part in range(3):
                pb, pk = banks[part]
                k.op("act", lambda: nc.scalar.copy(zin[part][:, 3:515], pb[:]), writes=[pk, zink[part]] + ([("T1", 0), ("T1", 1), ("Dec", 0), ("Dec", 1)] if part == 2 else []))
                yield
            for part in range(3):
                z, zk, ac, ak = zin[part], zink[part], acc[part], f"acc{part}"
                zal = [("T1", 0), ("T1", 1), ("Dec", 0), ("Dec", 1)] if part == 2 else []
                k.op("dve", lambda: nc.vector.tensor_copy(z[:, 0:3], carry[:, part, :]), reads=["carry"], writes=[zk] + zal)
                k.op("dve", lambda: nc.vector.tensor_copy(carry[:, part, :], z[:, 512:515]), reads=[zk], writes=["carry"])
                for jj in (3, 2, 1, 0):
                    wc = convw[:, jj * 24 + part * 8 + h:jj * 24 + part * 8 + h + 1]
                    if jj == 3:
                        k.op("dve", lambda: nc.vector.tensor_scalar(ac, z[:, 3:515], wc, None, ALU.mult), reads=["convw", zk], writes=[ak])
                        yield
                    else:
                        k.op("dve", lambda: nc.vector.scalar_tensor_tensor(ac, z[:, jj:jj + 512], wc, ac, ALU.mult, ALU.add),
                             reads=["convw", zk], writes=[ak])
                        yield
            pb, pk = banks[3]
            k.op("act", lambda: nc.scalar.activation(gT, pb[:], AF.Silu), writes=[pk, "gT"])
            k.op("act", lambda: nc.scalar.activation(acc[0], acc[0], AF.Silu), writes=["acc0"])
            yield
            k.op("act", lambda: nc.scalar.activation(acc[1], acc[1], AF.Silu), writes=["acc1"])
            k.op("act", lambda: nc.scalar.activation(vT, acc[2], AF.Silu), reads=["acc2"], writes=["vT"])
            for part in range(2):
                k.op("act", lambda: nc.scalar.activation(sq[part], acc[part], AF.Square), reads=[f"acc{part}"], writes=[f"sq{part}"])
            yield
            pbs = {}
            for part in range(2):
                pb, pk = self.bank()
                pbs[part] = (pb, pk)
                k.op("pe", lambda: nc.tensor.matmul(pb[:], self.ones_b[:], sq[part], start=True, stop=True), reads=["ones_b", f"sq{part}"], writes=[pk])
            for part in range(2):
                pb, pk = pbs[part]
                k.op("act", lambda: nc.scalar.activation(rs[part], pb[:], AF.Ln, bias=EPS), writes=[pk, f"rs{part}"])
            for part in range(2):
                k.op("act", lambda: nc.scalar.activation(rs[part], rs[part], AF.Exp, scale=-0.5), writes=[f"rs{part}"])
            yield
            k.op("dve", lambda: nc.vector.scalar_tensor_tensor(qT, acc[0], 128 ** -0.5, rs[0], ALU.mult, ALU.mult), reads=["acc0", "rs0"], writes=["qT"])
            yield
            k.op("dve", lambda: nc.vector.scalar_tensor_tensor(kT, acc[1], 1.0, rs[1], ALU.mult, ALU.mult), reads=["acc1", "rs1"], writes=["kT"])
            yield
            B4 = [64, 8, 128]
            for name, srcT in (("k", kT), ("v", vT), ("g", gT)):
                pb, pk = self.bank()
                pv = pb[:].bitcast(BF16).rearrange("p (a b) -> p a b", a=8, b=128)
                for cc in range(8):
                    k.op("pe", lambda cc=cc: nc.tensor.transpose(pv[0:64, cc, :], srcT[:, cc * 64:(cc + 1) * 64], self.ident_b[:]),
                         reads=[name + "T", "ident_b"], writes=[pk])
                if name == "g":
                    k.op("dve", lambda: nc.vector.tensor_tensor(sg2[u % 2], pv[0:64], onb[0:64, :].unsqueeze(1).to_broadcast(B4), ALU.mult),
                         reads=["onb"], writes=[pk, f"sg{u % 2}"])
                elif name == "k":
                    k.op("dve", lambda: nc.vector.tensor_tensor(kbg, pv[0:64], T["bg"][:, c0:c0 + 8, h:h + 1].to_broadcast(B4), ALU.mult),
                         reads=["bg"], writes=[pk, "kbg"])
                    k.op("dve", lambda: nc.vector.tensor_tensor(kdec2[u % 2], pv[0:64], T["kfac"][:, c0:c0 + 8, h:h + 1].to_broadcast(B4), ALU.mult),
                         reads=["kfac"], writes=[pk, f"kdec{u % 2}"])
                else:
                    k.op("dve", lambda: nc.vector.tensor_tensor(vb, pv[0:64], T["beta"][:, c0:c0 + 8, h:h + 1].to_broadcast(B4), ALU.mult),
                         reads=["beta"], writes=[pk, "vb"])
                yield

        def N(u):
            h, seg = units[u]
            t0 = seg * 512
            c0 = seg * 8
            hts = self.htk(seg)
            G4 = [64, 4, 64]

            def bc4(ap2d):
                return ap2d.unsqueeze(1).to_broadcast(G4)

            def v4(pbank, parts=64):
                return pbank[0:parts, 0:256].rearrange("p (c j) -> p c j", c=4, j=64)
            gs_ = [slice(4 * g, 4 * g + 4) for g in range(2)]
            for g in range(2):
                gl = gs_[g]
                cg = slice(c0 + 4 * g, c0 + 4 * g + 4)
                tsl = slice(256 * g, 256 * g + 256)
                gsl = T["g"][:, cg, h:h + 1].to_broadcast(G4)
                gcb = T["gc"][:, cg, h:h + 1].to_broadcast(G4)
                btb = T["beta"][:, cg, h:h + 1].to_broadcast(G4)
                k.op("dve", lambda: nc.vector.tensor_tensor(mt["T1"][:, gl, :], bc4(ltri64), gsl, ALU.mult), reads=["g", "ltri"], writes=[("T1", g)])
                pg, pgk = self.bank()
                for i in range(4):
                    k.op("pe", lambda i=i: nc.tensor.matmul(pg[:, i * 64:(i + 1) * 64], ones_f[0:64, :], mt["T1"][:, 4 * g + i, :], start=True, stop=True),
                         reads=["ones_f", ("T1", g)], writes=[pgk])
                pg3 = pg[:, 0:256].rearrange("p (c j) -> p c j", c=4, j=64)
                k.op("act", lambda: nc.scalar.activation(egl8[:, gl], pg3[:, :, 63], AF.Exp), writes=[pgk, ("egl8", g)])
                k.op("act", lambda: nc.scalar.activation(rs[0][:, tsl], pg[:, 0:256], AF.Exp), writes=[pgk, "rs0"])
                k.op("dve", lambda: nc.vector.tensor_tensor(qeT[:, tsl], qT[:, tsl], rs[0][:, tsl], ALU.mult), reads=["qT", "rs0"], writes=[("qeT", g)])
                k.op("dve", lambda: nc.vector.tensor_tensor(mt["T1"][:, gl, :], v4(pg), gcb, ALU.subtract), reads=["gc"], writes=[pgk, ("T1", g)])
                k.op("dve", lambda: nc.vector.tensor_tensor(mt["Dec"][:, gl, :], mt["T1"][:, gl, :], bc4(self.mposL[0:64, 0:64]), ALU.add),
                     reads=[("T1", g), "mposL"], writes=[("Dec", g)])
                k.op("act", lambda: nc.scalar.activation(mt["Dec"][:, gl, :], mt["Dec"][:, gl, :], AF.Exp, scale=-1.0), writes=[("Dec", g)])
                k.op("dve", lambda: nc.vector.tensor_tensor(mt["DecT"][:, gl, :], mt["T1"][:, gl, :], bc4(self.mposU[0:64, 0:64]), ALU.subtract),
                     reads=[("T1", g), "mposU"], writes=[("DecT", g)])
                k.op("act", lambda: nc.scalar.activation(mt["DecT"][:, gl, :], mt["DecT"][:, gl, :], AF.Exp), writes=[("DecT", g)])
                k.op("dve", lambda: nc.vector.tensor_tensor(mt["Dec"][:, gl, :], mt["Dec"][:, gl, :], bc4(self.strict[0:64, 0:64]), ALU.mult),
                     reads=["strict"], writes=[("Dec", g)])
                k.op("dve", lambda: nc.vector.tensor_tensor(mt["Dec"][:, gl, :], mt["Dec"][:, gl, :], btb, ALU.mult), reads=["beta"], writes=[("Dec", g)])
                yield
            for g in range(2):
                gl = gs_[g]
                pkk, pkkk = self.bank()
                for i in range(4):
                    cs = slice((4 * g + i) * 64, (4 * g + i + 1) * 64)
                    k.op("pe", lambda i=i, cs=cs: nc.tensor.matmul(pkk[0:64, i * 64:(i + 1) * 64], kT[:, cs], kT[:, cs], start=True, stop=True),
                         reads=["kT"], writes=[pkkk])
                k.op("dve", lambda: nc.vector.tensor_tensor(mt["M"][:, gl, :], v4(pkk), mt["Dec"][:, gl, :], ALU.mult), reads=[("Dec", g)], writes=[pkkk, ("M", g)])
                yield
            for g in range(2):
                gl = gs_[g]
                pt, ptk = self.bank()
                for i in range(4):
                    k.op("pe", lambda i=i: nc.tensor.transpose(pt[0:64, i * 64:(i + 1) * 64], mt["M"][:, 4 * g + i, :], idf), reads=[("M", g), "ident_f"], writes=[ptk])
                k.op("act", lambda: nc.scalar.copy(mt["Mt"][:, gl, :], v4(pt)), writes=[ptk, ("Mt", g)])
                k.op("dve", lambda: nc.vector.tensor_tensor(mt["Pt"][:, gl, :], bc4(idf), v4(pt), ALU.subtract), reads=["ident_f"], writes=[ptk, ("Pt", g)])
                yield
            cur = ("M", "Mt")
            for s_ in range(1, 6):
                nxt = ("A1", "At1") if cur[0] == "M" else ("M", "Mt")
                Ac, Atc = mt[cur[0]], mt[cur[1]]
                An, Atn = mt[nxt[0]], mt[nxt[1]]
                for g in range(2):
                    gl = gs_[g]
                    pa, pak = self.bank()
                    for i in range(4):
                        c_ = 4 * g + i
                        k.op("pe", lambda i=i, c_=c_: nc.tensor.matmul(pa[0:64, i * 64:(i + 1) * 64], Atc[:, c_, :], Ac[:, c_, :], start=True, stop=True),
                             reads=[(cur[0], g), (cur[1], g)], writes=[pak])
                    k.op("act", lambda: nc.scalar.copy(An[:, gl, :], v4(pa)), writes=[pak, (nxt[0], g)])
                    if s_ < 5:
                        pat, patk = self.bank()
                        for i in range(4):
                            c_ = 4 * g + i
                            k.op("pe", lambda i=i, c_=c_: nc.tensor.matmul(pat[0:64, i * 64:(i + 1) * 64], Ac[:, c_, :], Atc[:, c_, :], start=True, stop=True),
                                 reads=[(cur[0], g), (cur[1], g)], writes=[patk])
                        k.op("dve", lambda: nc.vector.tensor_copy(Atn[:, gl, :], v4(pat)), writes=[patk, (nxt[1], g)])
                    yield
                for g in range(2):
                    gl = gs_[g]
                    pp, ppk = self.bank()
                    for i in range(4):
                        c_ = 4 * g + i
                        k.op("pe", lambda i=i, c_=c_: nc.tensor.matmul(pp[0:64, i * 64:(i + 1) * 64], An[:, c_, :], mt["Pt"][:, c_, :], start=True, stop=True),
                             reads=[(nxt[0], g), ("Pt", g)], writes=[ppk])
                    if s_ < 5:
                        k.op("dve", lambda: nc.vector.tensor_tensor(mt["Pt"][:, gl, :], v4(pp), mt["Pt"][:, gl, :], ALU.add), writes=[ppk, ("Pt", g)])
                    else:
                        k.op("dve", lambda: nc.vector.tensor_tensor(Ttb[:, gl, :], v4(pp), mt["Pt"][:, gl, :], ALU.add), reads=[("Pt", g)], writes=[ppk, ("Ttb", g)])
                    yield
                cur = nxt
            for g in range(2):
                gl = gs_[g]
                pq, pqk = self.bank()
                for i in range(4):
                    cs = slice((4 * g + i) * 64, (4 * g + i + 1) * 64)
                    k.op("pe", lambda i=i, cs=cs: nc.tensor.matmul(pq[0:64, i * 64:(i + 1) * 64], kT[:, cs], qT[:, cs], start=True, stop=True),
                         reads=["kT", "qT"], writes=[pqk])
                k.op("dve", lambda: nc.vector.tensor_tensor(attnT[:, gl, :], v4(pq), mt["DecT"][:, gl, :], ALU.mult), reads=[("DecT", g)], writes=[pqk, ("attnT", g)])
                yield
            for g in range(2):
                gl = gs_[g]
                pu, puk = self.bank()
                for i in range(4):
                    c_ = 4 * g + i
                    k.op("pe", lambda c_=c_, i=i: nc.tensor.matmul(pu[0:64, i * 128:(i + 1) * 128], Ttb[:, c_, :], vb[:, c_, :], start=True, stop=True),
                         reads=[("Ttb", g), "vb"], writes=[puk])
                k.op("act", lambda: nc.scalar.copy(u_sb[:, gl, :], pu[0:64, :].rearrange("p (a b) -> p a b", a=4, b=128)),
                     writes=[puk, ("u", g)])
                pw, pwk = self.bank()
                for i in range(4):
                    c_ = 4 * g + i
                    k.op("pe", lambda c_=c_, i=i: nc.tensor.matmul(pw[:, i * 64:(i + 1) * 64], kbg[:, c_, :], Ttb[:, c_, :], start=True, stop=True),
                         reads=[("Ttb", g), "kbg"], writes=[pwk])
                k.op("dve", lambda: nc.vector.tensor_copy(wTb[:, gl, :], pw[:, 0:256].rearrange("p (c j) -> p c j", c=4, j=64)), writes=[pwk, ("wTb", g)])
                yield
        def C(u):
            h, seg = units[u]
            t0 = seg * 512
            c0 = seg * 8
            hts = self.htk(seg)
            if seg == 0:
                k.op("dve", lambda: nc.vector.memset(Sf, 0.0), writes=["Sf"])
                k.op("dve", lambda: nc.vector.memset(Sb, 0.0), writes=["Sb"])
            for cc in range(8):
                cs = slice(cc * 64, (cc + 1) * 64)
                p1, p1k = self.bank()
                g = cc // 4
                k.op("pe", lambda: nc.tensor.matmul(p1[0:64, 0:128], wTb[:, cc, :], Sb, start=True, stop=True), reads=[("wTb", g), "Sb"], writes=[p1k])
                k.op("dve", lambda: nc.vector.tensor_tensor(vnb, u_sb[:, cc, :], p1[0:64, 0:128], ALU.subtract), reads=[("u", g)], writes=[p1k, "vnb"])
                yield
                p4, p4k = self.bank()
                k.op("pe", lambda: nc.tensor.matmul(p4[:, 0:128], kdec2[u % 2][:, cc, :], vnb, start=True, stop=True), reads=[f"kdec{u % 2}", "vnb"], writes=[p4k])
                p3, p3k = self.bank()
                k.op("pe", lambda: nc.tensor.matmul(p3[0:64, 0:128], qeT[:, cs], Sb, start=True, stop=False), reads=[("qeT", g), "Sb"], writes=[p3k])
                k.op("pe", lambda: nc.tensor.matmul(p3[0:64, 0:128], attnT[:, cc, :], vnb, start=False, stop=True), reads=[("attnT", g), "vnb"], writes=[p3k])
                k.op("dve", lambda: nc.vector.scalar_tensor_tensor(Sb, Sf, egl8[:, cc:cc + 1], p4[:, 0:128], ALU.mult, ALU.add),
                     reads=[("egl8", g), "Sf"], writes=[p4k, "Sb"])
                k.op("dve", lambda: nc.vector.scalar_tensor_tensor(Sf, Sf, egl8[:, cc:cc + 1], p4[:, 0:128], ALU.mult, ALU.add),
                     reads=[("egl8", g)], writes=[p4k, "Sf"])
                k.op("act", lambda: nc.scalar.copy(o64[:, cc, :], p3[0:64, 0:128]), writes=[p3k, "o64"])
                yield

        def O(u):
            h, seg = units[u]
            Wo, wko = get_wo(h)
            t0 = seg * 512
            c0 = seg * 8
            hts = self.htk(seg)
            k.op("act", lambda: nc.scalar.activation(sqo, o64, AF.Square), reads=["o64"], writes=["acc0"])
            k.op("dve", lambda: nc.vector.tensor_reduce(ssg[:, 0:8], sqo, mybir.AxisListType.X, ALU.add), reads=["acc0"], writes=["ssg"])
            k.op("act", lambda: nc.scalar.activation(ssg[:, 8:16], ssg[:, 0:8], AF.Ln, scale=1.0 / 128, bias=EPS), writes=["ssg"])
            k.op("act", lambda: nc.scalar.activation(ssg[:, 16:24], ssg[:, 8:16], AF.Exp, scale=-0.5), writes=["ssg"])
            yield
            k.op("dve", lambda: nc.vector.tensor_tensor(o64, o64, ssg[:, 16:24].unsqueeze(2).to_broadcast(B4), ALU.mult), reads=["ssg"], writes=["o64"])
            k.op("dve", lambda: nc.vector.tensor_tensor(og, o64, sg2[u % 2], ALU.mult), reads=["o64", f"sg{u % 2}"], writes=["acc1"])
            yield
            pb, pk = self.bank()
            pv = pb[:].bitcast(BF16).rearrange("p (a b) -> p a b", a=16, b=64)
            for cc in range(8):
                k.op("pe", lambda cc=cc: nc.tensor.transpose(pv[:, cc, :], og[:, cc, :], idb64), reads=["acc1", "ident_b"], writes=[pk])
            cp(ogT.rearrange("p (a b) -> p a b", a=8, b=64), pv[:, 0:8, :], writes=[pk, "acc2"])
            Wov = Wo[:, 0:1024]
            for tt in range(4):
                for db in range(2):
                    pb, pk = self.bank()
                    k.op("pe", lambda: nc.tensor.matmul(pb[:], ogT[:, tt * 128:(tt + 1) * 128], Wov[:, db * 512:(db + 1) * 512], start=True, stop=True),
                         reads=["acc2", wko], writes=[pk])
                    self.add_to_x(pb, pk, seg * 4 + tt, db * 512)
                yield

        def run(*gens):
            gens = list(gens)
            while gens:
                for g_ in list(gens):
                    try:
                        next(g_)
                    except StopIteration:
                        gens.remove(g_)

        self.bank_lo = 4
        run(P(0))
        run(N(0))
        def seq(*gs):
            for g_ in gs:
                yield from g_

        for u in range(len(units)):
            if u + 1 < len(units):
                run(C(u), P(u + 1))
                run(O(u), N(u + 1))
            else:
                run(seq(C(u), O(u)))
        self.bank_lo = 0
        print("gdn scratch bytes", off[0])

    def emit_final(self, do_norm=True):
        k, nc = self.k, self.nc
        k.barrier()
        ob = [self.sv(i * 4096, [1, D], F32) for i in range(2)]
        ov = self.out_d.rearrange("(t p) d -> p t d", p=128)
        ss = self.ss
        if do_norm:
            k.dma("sp", self.Gb[:], self.w["final_norm"].partition_broadcast(128), writes=["Gb"])
            k.op("dve", lambda: nc.vector.memset(ss[:, 0:16], 0.0), writes=["ss"])
            for t in range(NT):
                k.op("act", lambda t=t: nc.scalar.activation(self.junk[:], self.X[:, t, :], AF.Square, accum_out=ss[:, t:t + 1]),
                     reads=[("X", t)], writes=["junk", "ss"])
            k.op("act", lambda: nc.scalar.activation(ss[:, 16:32], ss[:, 0:16], AF.Ln, scale=1.0 / D, bias=EPS), writes=["ss"])
            k.op("act", lambda: nc.scalar.activation(ss[:, 32:48], ss[:, 16:32], AF.Exp, scale=-0.5), writes=["ss"])
        for t in range(NT):
            if do_norm:
                o = ob[t % 2]
                okk = f"ob{t % 2}"
                k.op("dve", lambda t=t, o=o: nc.vector.scalar_tensor_tensor(o[:, 0, :], self.X[:, t, :], ss[:, 32 + t:33 + t], self.Gb[:],
                                                                           ALU.mult, ALU.mult), reads=[("X", t), "ss", "Gb"], writes=[okk])
                k.dma("sp", ov[:, t, :], o[:, 0, :], reads=[okk], sname=f"out{t % 2}")
            else:
                k.dma("sp", ov[:, t, :], self.X[:, t, :], reads=[("X", t)], sname="out")


FULL_PLAN = [("mla", 0, 0), ("xattn", 0), ("mlp", 0),
             ("gdn", 1, 0), ("xattn", 1), ("mlp", 1),
             ("sc", 2, 0), ("xattn", 2), ("mlp", 2),
             ("mla", 3, 1), ("xattn", 3), ("mlp", 3),
             ("final",)]


def run_plan(plan, inputs, cores):
    prog = Prog(plan)
    nc = prog.build()
    cst, _ = host_consts()
    in_maps = []
    for b in cores:
        m = {"x": np.ascontiguousarray(inputs["x"][b]), "mem": np.ascontiguousarray(inputs["mem"][b]),
             "pos": np.ascontiguousarray(inputs["positions"][b]).astype(np.int32), "cst": cst}
        for n in W_NAMES:
            m[n] = np.ascontiguousarray(np.asarray(inputs[n], dtype=np.float32))
        in_maps.append(m)
    res = run_bass_kernel_spmd(nc, in_maps, core_ids=list(range(len(cores))))
    return np.stack([r["out"] for r in res.results], axis=0)


def kernel(**inputs):
    inputs = {k: np.asarray(v) for k, v in inputs.items()}
    out = run_plan(FULL_PLAN, inputs, list(range(8)))
    return out.astype(np.float32)
```

```python
import numpy as np
import concourse.bass as bass
import concourse.mybir as mybir
from concourse.bass_utils import run_bass_kernel_spmd

F32 = mybir.dt.float32
BF16 = mybir.dt.bfloat16
I32 = mybir.dt.int32
AF = mybir.ActivationFunctionType
ALU = mybir.AluOpType

D = 1024
S = 2048
NT = 16
KC = 8
MEM = 256
EPS = 1e-6
SEM_EPOCH = 20000
BIG = 30000.0
SLOT_ELEMS = 4096
NSLOT = 3


class K:
    def __init__(self, nc):
        self.nc = nc
        self.E = {"pe": nc.tensor, "act": nc.scalar, "dve": nc.vector, "pool": nc.gpsimd, "sp": nc.sync}
        self.need = {e: set() for e in self.E}
        self.rank = {}
        self.esems = {e: [] for e in self.E}
        self.dma_sem_h = {}
        self.mode = "dry"
        self.reset()

    @property
    def dry(self):
        return self.mode == "dry"

    def reset(self):
        self.idx = {e: 0 for e in self.E}
        self.seen = {e: {} for e in self.E}
        self.last_w = {}
        self.readers = {}
        self.dma_cnt = {}
        self.rr = 0

    def finish_plan(self):
        for e in self.E:
            order = sorted(self.need[e])
            self.rank[e] = {}
            for r, i in enumerate(order):
                ep = r // SEM_EPOCH
                if ep >= len(self.esems[e]):
                    self.esems[e].append(self.nc.alloc_semaphore(f"s_{e}_{ep}"))
                self.rank[e][i] = (self.esems[e][ep], r % SEM_EPOCH + 1)

    def _collect(self, reads, writes):
        deps = {}

        def add(d):
            if d[0] not in deps or deps[d[0]][1] < d[1]:
                deps[d[0]] = d
        for k in reads:
            if k in self.last_w:
                add(self.last_w[k])
        for k in writes:
            if k in self.last_w:
                add(self.last_w[k])
            for d in self.readers.get(k, {}).values():
                add(d)
        return deps

    def _wait(self, eng, deps, skip_self=False):
        for name, (_, val) in deps.items():
            if skip_self and name == eng:
                continue
            if name.startswith("d_"):
                val = max(val, self.dma_cnt[name[2:]])
            if self.seen[eng].get(name, 0) >= val:
                continue
            self.seen[eng][name] = val
            if name.startswith("d_"):
                if self.mode == "emit":
                    self.E[eng].wait_ge(self.dma_sem_h[name[2:]], val)
            elif self.mode == "plan":
                self.need[name].add(val)
            else:
                sem, v = self.rank[name][val]
                self.E[eng].wait_ge(sem, v)

    def _record(self, d, reads, writes):
        for k in writes:
            self.last_w[k] = d
            self.readers[k] = {}
        for k in reads:
            if k in writes:
                continue
            self.readers.setdefault(k, {})[d[0]] = d

    def op(self, eng, fn, reads=(), writes=()):
        if self.mode == "dry":
            return None
        deps = self._collect(reads, writes)
        self._wait(eng, deps, skip_self=(eng == "pe"))
        self.idx[eng] += 1
        i = self.idx[eng]
        if self.mode == "emit":
            ins = fn()
            if i in self.rank[eng]:
                ins.then_inc(self.rank[eng][i][0], 1)
        self._record((eng, i), reads, writes)
        return None

    def dma(self, eng, out, in_, reads=(), writes=(), sname=None, **kw):
        if self.mode == "dry":
            return None
        serial = sname is None
        if serial:
            self.rr += 1
            sname = f"q_{eng}_{self.rr % 8}"
        if sname not in self.dma_sem_h:
            self.dma_sem_h[sname] = self.nc.alloc_semaphore(f"d_{sname}")
        self.dma_cnt.setdefault(sname, 0)
        deps = self._collect(reads, writes)
        if serial and self.dma_cnt[sname] > 0:
            deps[f"d_{sname}"] = (f"d_{sname}", self.dma_cnt[sname])
        self._wait(eng, deps)
        if self.mode == "emit":
            n0 = self.nc.n_instructions()
            ins = self.E[eng].dma_start(out=out, in_=in_, **kw)
            assert self.nc.n_instructions() - n0 == 1, "DMA was split"
            ins.then_inc(self.dma_sem_h[sname], 16)
        self.dma_cnt[sname] += 16
        self._record((f"d_{sname}", self.dma_cnt[sname]), reads, writes)
        return None

    def barrier(self):
        if self.mode == "dry":
            return
        alld = {}
        for e in self.E:
            if self.idx[e] > 0:
                alld[e] = (e, self.idx[e])
        for sname, val in self.dma_cnt.items():
            if val > 0:
                alld[f"d_{sname}"] = (f"d_{sname}", val)
        for e in self.E:
            self._wait(e, alld)
        self.last_w = {}
        self.readers = {}

    def final_wait(self, eng="sp"):
        alld = {}
        for sname, val in self.dma_cnt.items():
            if val > 0:
                alld[f"d_{sname}"] = (f"d_{sname}", val)
        self._wait(eng, alld)


def host_consts():
    c = {}
    p = np.arange(128)
    c["ident"] = np.eye(128, dtype=np.float32)
    c["ones"] = np.ones((128, 128), np.float32)
    c["maskC"] = (p[None, :] >= p[:, None]).astype(np.float32)
    same = (p[:, None] // 64) == (p[None, :] // 64)
    c["ltri"] = ((p[:, None] <= p[None, :]) & same).astype(np.float32)
    low = (p[None, :] <= p[:, None]) & same
    c["mposL"] = np.where(low, 0.0, BIG).astype(np.float32)
    c["mposU"] = np.where(low.T, 0.0, BIG).astype(np.float32)
    c["strict"] = ((p[None, :] < p[:, None]) & same).astype(np.float32)
    c["p2"] = ((p[:, None] % 64) == (p[None, :] % 64)).astype(np.float32)
    inv = (10000.0 ** (-np.arange(0, 64, 2, dtype=np.float32) / 64)).astype(np.float32)
    rc = np.zeros((128, 128), np.float32)
    rc[:, 0] = (inv[p % 32].astype(np.float64) / (2 * np.pi)).astype(np.float32)
    rc[:, 1] = np.where(p < 64, 0.25, np.where(p < 96, 0.5, 0.0))
    c["rc"] = rc
    names = ["ident", "ones", "maskC", "ltri", "mposL", "mposU", "strict", "p2", "rc"]
    return np.concatenate([c[n] for n in names], axis=1), names


W_NAMES = ["mla_w_in", "mla_q_norm", "mla_kv_norm", "mla_w_uq", "mla_w_ukv", "mla_w_o",
           "gdn_w_in", "gdn_conv_w", "gdn_a_log", "gdn_dt_bias", "gdn_o_norm", "gdn_w_o",
           "sc_w_in", "sc_conv_w", "sc_w_o", "norm_mix", "norm_mem", "norm_mlp",
           "xa_w_q", "xa_w_kv", "xa_w_o", "mlp_w1", "mlp_w2", "mem_norm", "final_norm"]
W_SHAPES = {
    "mla_w_in": [2, 1024, 704], "mla_q_norm": [2, 384], "mla_kv_norm": [2, 256],
    "mla_w_uq": [2, 384, 1536], "mla_w_ukv": [2, 256, 2048], "mla_w_o": [2, 1024, 1024],
    "gdn_w_in": [1, 1024, 4112], "gdn_conv_w": [1, 4, 3072], "gdn_a_log": [1, 8], "gdn_dt_bias": [1, 8],
    "gdn_o_norm": [1, 128], "gdn_w_o": [1, 1024, 1024],
    "sc_w_in": [1, 1024, 3072], "sc_conv_w": [1, 3, 1024], "sc_w_o": [1, 1024, 1024],
    "norm_mix": [4, 1024], "norm_mem": [4, 1024], "norm_mlp": [4, 1024],
    "xa_w_q": [4, 1024, 1024], "xa_w_kv": [4, 1024, 2048], "xa_w_o": [4, 1024, 1024],
    "mlp_w1": [4, 1024, 4096], "mlp_w2": [4, 4096, 1024], "mem_norm": [1024], "final_norm": [1024],
}


class Prog:
    def __init__(self, plan):
        self.plan = plan
        nc = bass.Bass("TRN2", target_bir_lowering=False)
        self.nc = nc
        self.k = K(nc)
        dt = nc.dram_tensor
        self.x_d = dt("x", [S, D], F32, kind="ExternalInput").ap()
        self.mem_d = dt("mem", [MEM, D], F32, kind="ExternalInput").ap()
        self.pos_d = dt("pos", [S], I32, kind="ExternalInput").ap()
        cst, names = host_consts()
        self.cst_d = dt("cst", list(cst.shape), F32, kind="ExternalInput").ap()
        self.cnames = names
        self.w = {n: dt(n, W_SHAPES[n], F32, kind="ExternalInput").ap() for n in W_NAMES}
        self.out_d = dt("out", [S, D], F32, kind="ExternalOutput").ap()
        self._alloc()
        self.pieces = []
        self.piece_ptr = 0
        self.issued = 0

    def _alloc(self):
        nc = self.nc
        a = nc.alloc_sbuf_tensor
        self.X = a("X", [128, NT, D], F32)
        self.HT = a("HT", [128, KC, S], BF16)
        self.W = [a(f"W{i}", [128, SLOT_ELEMS], BF16) for i in range(NSLOT)]
        self.ident_f = a("ident_f", [128, 128], F32)
        self.ident_b = a("ident_b", [128, 128], BF16)
        self.ones_b = a("ones_b", [128, 128], BF16)
        self.maskC = a("maskC", [128, 128], BF16)
        self.ltri = a("ltri", [128, 128], F32)
        self.mposL = a("mposL", [128, 128], F32)
        self.mposU = a("mposU", [128, 128], F32)
        self.strict = a("strict", [128, 128], F32)
        self.p2 = a("p2", [128, 128], BF16)
        self.rc = a("rc", [128, 128], F32)
        self.Gb = a("Gb", [128, D], F32)
        self.memnT = a("memnT", [128, KC, MEM], BF16)
        self.junk = a("junk", [128, D], BF16)
        self.hn = [a(f"hn{i}", [128, D], BF16) for i in range(2)]
        self.ss = a("ss", [128, 64], F32)
        self.small = a("small", [128, 256], F32)
        self.stage = a("stage", [128, 128], F32)
        self.stage2 = a("stage2", [128, 128], F32)
        self.SCRB = 55296 + 8192 + 5632
        self.scr = a("scr", [128, self.SCRB // 4], F32)
        self.ps = [nc.alloc_psum_tensor(f"ps{i}", [128, 512], F32) for i in range(8)]
        self.ps_i = 0
        print("sbuf bytes remaining", nc.sbuf_bytes_remaining)

    def sv(self, off, shape, dtype):
        esz = 4 if dtype in (F32, I32) else 2
        n = int(np.prod(shape))
        assert off % 4 == 0 and off + n * esz <= self.SCRB, (off, shape)
        ap = self.scr[:, off // 4: off // 4 + (n * esz) // 4]
        if dtype != F32:
            ap = ap.bitcast(dtype)
        if len(shape) == 2:
            ap = ap.rearrange("p (a b) -> p a b", a=shape[0], b=shape[1])
        elif len(shape) == 3:
            ap = ap.rearrange("p (a b c) -> p a b c", a=shape[0], b=shape[1], c=shape[2])
        return ap

    def bank(self, lo=None, hi=8):
        lo = getattr(self, "bank_lo", 0) if lo is None else lo
        i = lo + (self.ps_i % (hi - lo))
        self.ps_i += 1
        return self.ps[i], f"ps{i}"

    def wget(self, piece):
        k = self.k
        if k.dry:
            self.pieces.append(piece)
            return self.W[0], "W0"
        i = self.piece_ptr
        assert self.pieces[i] == piece, (self.pieces[i], piece)
        while self.issued < min(len(self.pieces), i + NSLOT - 1):
            self._load(self.issued)
            self.issued += 1
        self.piece_ptr += 1
        return self.W[i % NSLOT], f"W{i % NSLOT}"

    def _load(self, i):
        piece = self.pieces[i]
        slot = i % NSLOT
        Wt = self.W[slot]
        key = f"W{slot}"
        k = self.k
        w = self.w

        def ld(dst, src, **kw):
            k.dma("pool", dst, src, writes=[key], sname=f"w{slot}", **kw)

        def rows(ap2d):
            return ap2d.rearrange("(kc p) c -> p kc c", p=128)

        kind = piece[0]
        if kind == "cols":
            _, name, l, c0, ncol = piece
            src = w[name][l]
            nk = src.shape[0] // 128
            dst = Wt[:, 0:nk * ncol].rearrange("p (kc c) -> p kc c", kc=nk, c=ncol)
            ld(dst, rows(src[:, c0:c0 + ncol]))
        elif kind == "rows":
            _, name, l, r0, nr = piece
            src = w[name][l]
            nk = nr // 128
            ncol = src.shape[1]
            dst = Wt[:, 0:nk * ncol].rearrange("p (kc c) -> p kc c", kc=nk, c=ncol)
            ld(dst, rows(src[r0:r0 + nr, :]))
        elif kind == "multi":
            _, name, l, segs = piece
            src = w[name][l]
            nk = src.shape[0] // 128
            tot = sum(n for _, n in segs)
            dst = Wt[:, 0:nk * tot].rearrange("p (kc c) -> p kc c", kc=nk, c=tot)
            o = 0
            for c0, n in segs:
                ld(dst[:, :, o:o + n], rows(src[:, c0:c0 + n]))
                o += n
        elif kind == "mla_head":
            _, j, h = piece
            uq = w["mla_w_uq"][j]
            ukv = w["mla_w_ukv"][j]
            dq = Wt[:, 0:768].rearrange("p (kc c) -> p kc c", kc=3, c=256)
            b = h * 192
            ld(dq[:, :, 0:192], rows(uq[:, b:b + 192]))
            ld(dq[:, :, 192:224], rows(uq[:, b + 160:b + 192]))
            ld(dq[:, :, 224:256], rows(uq[:, b + 128:b + 160]))
            dkv = Wt[:, 768:1280].rearrange("p (kc c) -> p kc c", kc=2, c=256)
            ld(dkv, rows(ukv[:, h * 256:(h + 1) * 256]))
        else:
            raise ValueError(piece)

    def build(self):
        k = self.k
        k.mode = "dry"
        self._emit_all()
        for mode in ("plan", "emit"):
            k.mode = mode
            k.reset()
            self.ps_i = 0
            self.piece_ptr = 0
            self.issued = 0
            self._emit_all()
            k.final_wait("sp")
            if mode == "plan":
                k.finish_plan()
        return self.nc

    def _emit_all(self):
        self.emit_setup()
        for st in self.plan:
            getattr(self, "emit_" + st[0])(*st[1:])

    def emit_setup(self):
        k, nc = self.k, self.nc
        cd = self.cst_d
        ci = {n: i for i, n in enumerate(self.cnames)}

        def cload(eng, dst, name):
            i = ci[name]
            k.dma(eng, dst[:], cd[:, i * 128:(i + 1) * 128], writes=[dst.name])
        cload("sp", self.ident_f, "ident")
        cload("pool", self.ident_b, "ident")
        cload("pool", self.ones_b, "ones")
        cload("pool", self.maskC, "maskC")
        cload("sp", self.ltri, "ltri")
        cload("sp", self.mposL, "mposL")
        cload("sp", self.mposU, "mposU")
        cload("sp", self.strict, "strict")
        cload("pool", self.p2, "p2")
        cload("sp", self.rc, "rc")
        xv = self.x_d.rearrange("(t p) d -> p t d", p=128)
        for t in range(NT):
            k.dma("sp", self.X[:, t, :], xv[:, t, :], writes=[("X", t)])
        mm = self.sv(0, [2, D], F32)
        memv = self.mem_d.rearrange("(t p) d -> p t d", p=128)
        for t in range(2):
            k.dma("sp", mm[:, t, :], memv[:, t, :], writes=["mm"])
        self.norm_T(self.w["mem_norm"], lambda t: mm[:, t, :], lambda t: "mm", 2,
                    lambda t: self.memnT[:, :, t * 128:(t + 1) * 128], lambda t: "memnT")
        k.barrier()

    def load_cols(self, dst_ap, src2d, n, key):
        k, nc = self.k, self.nc
        k.dma("sp", self.stage[0:n, :], src2d, writes=["stage"])
        pb, pk = self.bank()
        k.op("pe", lambda: nc.tensor.transpose(pb[:, 0:n], self.stage[0:n, :], self.ident_f[0:n, 0:n]),
             reads=["stage", "ident_f"], writes=[pk])
        k.op("dve", lambda: nc.vector.tensor_copy(dst_ap, pb[:, 0:n]), writes=[pk, key])

    def norm_T(self, gain_ap, src, src_key, ntile, dst, dst_key):
        k, nc = self.k, self.nc
        ss = self.ss
        k.dma("sp", self.Gb[:], gain_ap.partition_broadcast(128), writes=["Gb"])
        k.op("dve", lambda: nc.vector.memset(ss[:, 0:16], 0.0), writes=["ss"])
        for t in range(ntile):
            k.op("act", lambda t=t: nc.scalar.activation(self.junk[:], src(t), AF.Square, accum_out=ss[:, t:t + 1]),
                 reads=[src_key(t)], writes=["junk", "ss"])
        k.op("act", lambda: nc.scalar.activation(ss[:, 16:16 + ntile], ss[:, 0:ntile], AF.Ln, scale=1.0 / D, bias=EPS),
             writes=["ss"])
        k.op("act", lambda: nc.scalar.activation(ss[:, 32:32 + ntile], ss[:, 16:16 + ntile], AF.Exp, scale=-0.5),
             writes=["ss"])
        for t in range(ntile):
            hb = self.hn[t % 2]
            hk = f"hn{t % 2}"
            k.op("dve", lambda t=t, hb=hb: nc.vector.scalar_tensor_tensor(hb[:], src(t), ss[:, 32 + t:33 + t], self.Gb[:],
                                                                           ALU.mult, ALU.mult),
                 reads=[src_key(t), "ss", "Gb"], writes=[hk])
            pb, pk = self.bank()
            pv = pb[:].bitcast(BF16).rearrange("p (a b) -> p a b", a=8, b=128)
            for c in range(KC):
                k.op("pe", lambda c=c, hb=hb, pv=pv: nc.tensor.transpose(pv[:, c, :], hb[:, c * 128:(c + 1) * 128], self.ident_b[:]),
                     reads=[hk, "ident_b"], writes=[pk])
            if t % 2 == 0:
                k.op("act", lambda t=t, pv=pv: nc.scalar.copy(dst(t), pv), writes=[pk, dst_key(t)])
            else:
                k.op("dve", lambda t=t, pv=pv: nc.vector.tensor_copy(dst(t), pv), writes=[pk, dst_key(t)])

    def norm_x(self, gain_ap):
        self.norm_T(gain_ap, lambda t: self.X[:, t, :], lambda t: ("X", t), NT,
                    lambda t: self.HT[:, :, t * 128:(t + 1) * 128], lambda t: ("HT", t))

    def htk(self, tb):
        return [("HT", 4 * tb + i) for i in range(4)]

    def add_to_x(self, pb, pk, tt, c0, n=512):
        k, nc = self.k, self.nc
        xs = self.X[:, tt, c0:c0 + n]
        k.op("dve", lambda: nc.vector.tensor_tensor(xs, pb[:, 0:n], xs, ALU.add), writes=[pk, ("X", tt)])

    def out_proj(self, src, src_key, name, l):
        k, nc = self.k, self.nc
        for j in range(2):
            Wt, wk = self.wget(("cols", name, l, j * 512, 512))
            Wv = Wt[:, 0:4096].rearrange("p (kc c) -> p kc c", kc=8, c=512)
            for tt in range(NT):
                pb, pk = self.bank()
                for c in range(KC):
                    k.op("pe", lambda c=c, pb=pb, tt=tt: nc.tensor.matmul(pb[:], src[:, c, tt * 128:(tt + 1) * 128], Wv[:, c, :],
                                                                         start=(c == 0), stop=(c == KC - 1)),
                         reads=[src_key(tt), wk], writes=[pk])
                self.add_to_x(pb, pk, tt, j * 512)

    def emit_mlp(self, l):
        k, nc = self.k, self.nc
        self.norm_x(self.w["norm_mlp"][l])
        k.barrier()
        H = [self.sv(i * 16384, [4, S], BF16) for i in range(2)]
        tmp = [self.sv(32768 + i * 1024, [1, 512], BF16) for i in range(2)]
        n = 0
        for g in range(8):
            W1, k1 = self.wget(("cols", "mlp_w1", l, g * 512, 512))
            W1v = W1[:, 0:4096].rearrange("p (kc c) -> p kc c", kc=8, c=512)
            W2, k2 = self.wget(("rows", "mlp_w2", l, g * 512, 512))
            W2v = W2[:, 0:4096].rearrange("p (kc c) -> p kc c", kc=4, c=1024)
            Hg = H[g % 2]
            hk = f"H{g % 2}"
            for fc in range(4):
                for tb in range(4):
                    pb, pk = self.bank()
                    for c in range(KC):
                        k.op("pe", lambda c=c, pb=pb, fc=fc, tb=tb: nc.tensor.matmul(
                            pb[:], W1v[:, c, fc * 128:(fc + 1) * 128], self.HT[:, c, tb * 512:(tb + 1) * 512],
                            start=(c == 0), stop=(c == KC - 1)), reads=[k1] + self.htk(tb), writes=[pk])
                    tm = tmp[n % 2]
                    tk = f"tmp{n % 2}"
                    n += 1
                    k.op("act", lambda pb=pb, tm=tm: nc.scalar.activation(tm[:, 0, :], pb[:], AF.Relu), writes=[pk, tk])
                    k.op("dve", lambda tm=tm, fc=fc, tb=tb, Hg=Hg: nc.vector.tensor_tensor(
                        Hg[:, fc, tb * 512:(tb + 1) * 512], tm[:, 0, :], tm[:, 0, :], ALU.mult),
                        reads=[tk], writes=[(hk, fc, tb)])
            for tt in range(NT):
                for db in range(2):
                    pb, pk = self.bank()
                    for fc in range(4):
                        k.op("pe", lambda fc=fc, pb=pb, tt=tt, db=db, Hg=Hg: nc.tensor.matmul(
                            pb[:], Hg[:, fc, tt * 128:(tt + 1) * 128], W2v[:, fc, db * 512:(db + 1) * 512],
                            start=(fc == 0), stop=(fc == 3)), reads=[k2, (hk, fc, tt // 4)], writes=[pk])
                    self.add_to_x(pb, pk, tt, db * 512)

    def emit_xattn(self, l):
        k, nc = self.k, self.nc
        self.norm_x(self.w["norm_mem"][l])
        k.barrier()
        QT = self.sv(0, [8, S], BF16)
        KT = self.sv(32768, [8, MEM], BF16)
        V = self.sv(36864, [2, D], BF16)
        PT = [self.sv(40960 + i * 1024, [1, 512], BF16) for i in range(4)]
        rec = self.sv(45056, [1, 512], F32)
        scale = 256 ** -0.5
        for j in range(4):
            Wt, wk = self.wget(("cols", "xa_w_kv", l, j * 512, 512))
            Wv = Wt[:, 0:4096].rearrange("p (kc c) -> p kc c", kc=8, c=512)
            if j < 2:
                for cc in range(4):
                    pb, pk = self.bank()
                    for c in range(KC):
                        k.op("pe", lambda c=c, pb=pb, cc=cc: nc.tensor.matmul(pb[:, 0:MEM], Wv[:, c, cc * 128:(cc + 1) * 128], self.memnT[:, c, :],
                                                                             start=(c == 0), stop=(c == KC - 1)),
                             reads=[wk, "memnT"], writes=[pk])
                    k.op("act", lambda pb=pb, j=j, cc=cc: nc.scalar.copy(KT[:, j * 4 + cc, :], pb[:, 0:MEM]), writes=[pk, "KT"])
            else:
                for kt in range(2):
                    pb, pk = self.bank()
                    for c in range(KC):
                        k.op("pe", lambda c=c, pb=pb, kt=kt: nc.tensor.matmul(pb[:], self.memnT[:, c, kt * 128:(kt + 1) * 128], Wv[:, c, :],
                                                                             start=(c == 0), stop=(c == KC - 1)),
                             reads=[wk, "memnT"], writes=[pk])
                    k.op("act", lambda pb=pb, j=j, kt=kt: nc.scalar.copy(V[:, kt, (j - 2) * 512:(j - 1) * 512], pb[:]), writes=[pk, "V"])
        n = 0
        for j in range(2):
            Wt, wk = self.wget(("cols", "xa_w_q", l, j * 512, 512))
            Wv = Wt[:, 0:4096].rearrange("p (kc c) -> p kc c", kc=8, c=512)
            for cc in range(4):
                for tb in range(4):
                    pb, pk = self.bank()
                    for c in range(KC):
                        k.op("pe", lambda c=c, pb=pb, cc=cc, tb=tb: nc.tensor.matmul(
                            pb[:], Wv[:, c, cc * 128:(cc + 1) * 128], self.HT[:, c, tb * 512:(tb + 1) * 512],
                            start=(c == 0), stop=(c == KC - 1)), reads=[wk] + self.htk(tb), writes=[pk])
                    dst = QT[:, j * 4 + cc, tb * 512:(tb + 1) * 512]
                    if n % 2 == 0:
                        k.op("act", lambda pb=pb, dst=dst: nc.scalar.copy(dst, pb[:]), writes=[pk, ("QT", j * 4 + cc, tb)])
                    else:
                        k.op("dve", lambda pb=pb, dst=dst: nc.vector.tensor_copy(dst, pb[:]), writes=[pk, ("QT", j * 4 + cc, tb)])
                    n += 1
        cnt = [0]

        def xa_scores(h, tb):
            pts = []
            for kt in range(2):
                pb, pk = self.bank()
                for c in range(2):
                    k.op("pe", lambda c=c, pb=pb, kt=kt: nc.tensor.matmul(
                        pb[:], KT[:, h * 2 + c, kt * 128:(kt + 1) * 128], QT[:, h * 2 + c, tb * 512:(tb + 1) * 512],
                        start=(c == 0), stop=(c == 1)), reads=["KT", ("QT", h * 2 + c, tb)], writes=[pk])
                pt = PT[cnt[0] % 4]
                ptk = f"PT{cnt[0] % 4}"
                cnt[0] += 1
                k.op("act", lambda pb=pb, pt=pt: nc.scalar.activation(pt[:, 0, :], pb[:], AF.Exp, scale=scale), writes=[pk, ptk])
                pts.append((pt, ptk))
            return pts

        def xa_rest(h, tb, pts):
            pb, pk = self.bank()
            for kt in range(2):
                k.op("pe", lambda kt=kt, pb=pb: nc.tensor.matmul(pb[:], self.ones_b[:], pts[kt][0][:, 0, :], start=(kt == 0), stop=(kt == 1)),
                     reads=["ones_b", pts[kt][1]], writes=[pk])
            k.op("act", lambda pb=pb: nc.scalar.activation(rec[:, 0, :], pb[:], AF.Ln), writes=[pk, "rec"])
            k.op("act", lambda: nc.scalar.activation(rec[:, 0, :], rec[:, 0, :], AF.Exp, scale=-1.0), writes=["rec"])
            for c in range(2):
                pb, pk = self.bank()
                for kt in range(2):
                    k.op("pe", lambda kt=kt, pb=pb, c=c: nc.tensor.matmul(
                        pb[:], V[:, kt, h * 256 + c * 128:h * 256 + (c + 1) * 128], pts[kt][0][:, 0, :],
                        start=(kt == 0), stop=(kt == 1)), reads=["V", pts[kt][1]], writes=[pk])
                k.op("dve", lambda pb=pb, c=c: nc.vector.tensor_tensor(
                    self.HT[:, h * 2 + c, tb * 512:(tb + 1) * 512], pb[:], rec[:, 0, :], ALU.mult),
                    reads=["rec"], writes=[pk] + self.htk(tb))

        items = [(h, tb) for h in range(4) for tb in range(4)]
        prev = xa_scores(*items[0])
        for i, it in enumerate(items):
            nxt = xa_scores(*items[i + 1]) if i + 1 < len(items) else None
            xa_rest(it[0], it[1], prev)
            prev = nxt
        self.out_proj(self.HT, lambda tt: ("HT", tt), "xa_w_o", l)

    def emit_sc(self, l, j):
        k, nc = self.k, self.nc
        self.norm_x(self.w["norm_mix"][l])
        k.barrier()
        YT = self.sv(0, [8, S], BF16)
        uv = self.sv(32768, [1, S], F32)
        cu = self.sv(40960, [1, S + 2], F32)
        bb = self.sv(49168, [1, S], BF16)
        cw = self.small[:, 0:24].rearrange("p (j c) -> p c j", c=8, j=3)
        self.load_cols(self.small[:, 0:24], self.w["sc_conv_w"][j].rearrange("j (c p) -> (j c) p", p=128), 24, "cw")
        k.op("dve", lambda: nc.vector.memset(cu[:, 0, 0:2], 0.0), writes=["cu"])
        for cc in range(8):
            Wt, wk = self.wget(("multi", "sc_w_in", j, ((cc * 128, 128), (1024 + cc * 128, 128), (2048 + cc * 128, 128))))
            Wv = Wt[:, 0:3072].rearrange("p (kc c) -> p kc c", kc=8, c=384)
            for part in (2, 1, 0):
                for tb in range(4):
                    pb, pk = self.bank()
                    for c in range(KC):
                        k.op("pe", lambda c=c, pb=pb, tb=tb, part=part: nc.tensor.matmul(
                            pb[:], Wv[:, c, part * 128:(part + 1) * 128], self.HT[:, c, tb * 512:(tb + 1) * 512],
                            start=(c == 0), stop=(c == KC - 1)), reads=[wk] + self.htk(tb), writes=[pk])
                    sl = slice(tb * 512, (tb + 1) * 512)
                    if part == 2:
                        k.op("act", lambda pb=pb, sl=sl: nc.scalar.copy(uv[:, 0, sl], pb[:]), writes=[pk, "uv"])
                    elif part == 1:
                        k.op("dve", lambda pb=pb, tb=tb, sl=sl: nc.vector.tensor_tensor(
                            cu[:, 0, 2 + tb * 512:2 + (tb + 1) * 512], pb[:], uv[:, 0, sl], ALU.mult), reads=["uv"], writes=[pk, "cu"])
                    else:
                        k.op("act", lambda pb=pb, sl=sl: nc.scalar.copy(bb[:, 0, sl], pb[:]), writes=[pk, "bb"])
            k.op("dve", lambda cc=cc: nc.vector.tensor_scalar(uv[:, 0, :], cu[:, 0, 2:S + 2], cw[:, cc, 2:3], None, ALU.mult),
                 reads=["cu", "cw"], writes=["uv"])
            k.op("dve", lambda cc=cc: nc.vector.scalar_tensor_tensor(uv[:, 0, :], cu[:, 0, 1:S + 1], cw[:, cc, 1:2], uv[:, 0, :], ALU.mult, ALU.add),
                 reads=["cu", "cw"], writes=["uv"])
            k.op("dve", lambda cc=cc: nc.vector.scalar_tensor_tensor(uv[:, 0, :], cu[:, 0, 0:S], cw[:, cc, 0:1], uv[:, 0, :], ALU.mult, ALU.add),
                 reads=["cu", "cw"], writes=["uv"])
            k.op("dve", lambda cc=cc: nc.vector.tensor_tensor(YT[:, cc, :], uv[:, 0, :], bb[:, 0, :], ALU.mult),
                 reads=["uv", "bb"], writes=[("YT", cc)])
        self.out_proj_generic(YT, "sc_w_o", j)

    def out_proj_generic(self, src, name, l):
        k, nc = self.k, self.nc
        for j in range(2):
            Wt, wk = self.wget(("cols", name, l, j * 512, 512))
            Wv = Wt[:, 0:4096].rearrange("p (kc c) -> p kc c", kc=8, c=512)
            for tt in range(NT):
                pb, pk = self.bank()
                for c in range(KC):
                    k.op("pe", lambda c=c, pb=pb, tt=tt: nc.tensor.matmul(pb[:], src[:, c, tt * 128:(tt + 1) * 128], Wv[:, c, :],
                                                                         start=(c == 0), stop=(c == KC - 1)),
                         reads=[("YT", c), wk], writes=[pk])
                self.add_to_x(pb, pk, tt, j * 512)

    def emit_cs(self):
        k, nc = self.k, self.nc
        self.CS = self.sv(55296, [S], F32)
        pos_i = self.sv(0, [S], I32)
        tf = self.sv(8192, [S], F32)
        ti = self.sv(16384, [S], I32)
        tg = self.sv(24576, [S], F32)
        k.dma("sp", pos_i, self.pos_d.partition_broadcast(128), writes=["pos_i"])
        k.op("dve", lambda: nc.vector.tensor_copy(tf, pos_i), reads=["pos_i"], writes=["tf"])
        k.op("dve", lambda: nc.vector.tensor_scalar(tf, tf, self.rc[:, 0:1], self.rc[:, 1:2], ALU.mult, ALU.add),
             reads=["rc"], writes=["tf"])
        k.op("dve", lambda: nc.vector.tensor_copy(ti, tf), reads=["tf"], writes=["ti"])
        k.op("dve", lambda: nc.vector.tensor_copy(tg, ti), reads=["ti"], writes=["tg"])
        k.op("dve", lambda: nc.vector.tensor_tensor(tf, tf, tg, ALU.subtract), reads=["tg"], writes=["tf"])
        k.op("act", lambda: nc.scalar.activation(self.CS, tf, AF.Sin, scale=float(2 * np.pi)), reads=["tf"], writes=["CS"])
        k.barrier()

    def emit_mla(self, l, j):
        k, nc = self.k, self.nc
        self.norm_x(self.w["norm_mix"][l])
        k.barrier()
        self.emit_cs()
        cqnT = self.sv(0, [3, S], BF16)
        ckvnT = self.sv(12288, [2, S], BF16)
        KR2 = self.sv(20480, [1, S], BF16)
        QN = self.sv(24576, [1, S], BF16)
        QR = self.sv(28672, [1, S], BF16)
        KN = self.sv(32768, [1, S], BF16)
        Vh = self.sv(36864, [NT, 128], BF16)
        PT = [self.sv(40960 + i * 1024, [1, 512], BF16) for i in range(2)]
        rec = self.sv(43008, [1, 512], F32)
        zc = [self.sv(45056 + i * 2048, [1, 512], F32) for i in range(2)]
        sq = self.sv(49152, [5, 512], BF16)
        tk = self.sv(54272, [1, 512], BF16)
        gq = self.small[:, 32:35]
        gkv = self.small[:, 36:38]
        self.load_cols(gq, self.w["mla_q_norm"][j].rearrange("(f p) -> f p", p=128), 3, "gq")
        self.load_cols(gkv, self.w["mla_kv_norm"][j].rearrange("(f p) -> f p", p=128), 2, "gq")
        WA, ka = self.wget(("cols", "mla_w_in", j, 0, 512))
        WAv = WA[:, 0:4096].rearrange("p (kc c) -> p kc c", kc=8, c=512)
        WB, kb = self.wget(("multi", "mla_w_in", j, ((512, 192), (672, 32), (640, 32))))
        WBv = WB[:, 0:2048].rearrange("p (kc c) -> p kc c", kc=8, c=256)
        scale = 192 ** -0.5
        nz = 0
        for tb in range(4):
            sl = slice(tb * 512, (tb + 1) * 512)
            zlist = []
            for f in range(6):
                pb, pk = self.ps[f], f"ps{f}"
                if f < 4:
                    lw = lambda c, f=f: WAv[:, c, f * 128:(f + 1) * 128]
                    wk = ka
                else:
                    lw = lambda c, f=f: WBv[:, c, (f - 4) * 128:(f - 3) * 128]
                    wk = kb
                for c in range(KC):
                    k.op("pe", lambda c=c, pb=pb, lw=lw: nc.tensor.matmul(pb[:], lw(c), self.HT[:, c, sl], start=(c == 0), stop=(c == KC - 1)),
                         reads=[wk] + self.htk(tb), writes=[pk])
                if f < 5:
                    k.op("act", lambda pb=pb, f=f: nc.scalar.activation(sq[:, f, :], pb[:], AF.Square), writes=[pk, ("sq", f)])
                    zlist.append((pb, pk))
                else:
                    k.op("dve", lambda pb=pb: nc.vector.tensor_tensor(tk[:, 0, :], pb[:], self.CS[:, sl], ALU.mult), reads=["CS"], writes=[pk, "tk"])
                    pb2, pk2 = self.ps[6], "ps6"
                    k.op("pe", lambda pb2=pb2: nc.tensor.matmul(pb2[:], self.p2[:], tk[:, 0, :], start=True, stop=True), reads=["p2", "tk"], writes=[pk2])
                    k.op("act", lambda pb2=pb2: nc.scalar.copy(KR2[:, 0, sl], pb2[:]), writes=[pk2, "KR2"])
            for (f0, f1, nf, gg, dstT, dk) in ((0, 3, 384, gq, cqnT, "cqnT"), (3, 5, 256, gkv, ckvnT, "ckvnT")):
                pb, pk = self.ps[7], "ps7"
                for f in range(f0, f1):
                    k.op("pe", lambda f=f, pb=pb: nc.tensor.matmul(pb[:], self.ones_b[:], sq[:, f, :], start=(f == f0), stop=(f == f1 - 1)),
                         reads=["ones_b", ("sq", f)], writes=[pk])
                k.op("act", lambda pb=pb: nc.scalar.activation(rec[:, 0, :], pb[:], AF.Ln, scale=1.0 / nf, bias=EPS), writes=[pk, "rec"])
                k.op("act", lambda: nc.scalar.activation(rec[:, 0, :], rec[:, 0, :], AF.Exp, scale=-0.5), writes=["rec"])
                for f in range(f0, f1):
                    zb, zk = zlist[f]
                    z = zc[nz % 2]
                    zkk = f"zc{nz % 2}"
                    nz += 1
                    k.op("act", lambda zb=zb, z=z: nc.scalar.copy(z[:, 0, :], zb[:]), writes=[zk, zkk])
                    k.op("dve", lambda z=z, f=f, f0=f0, gg=gg, dstT=dstT: nc.vector.scalar_tensor_tensor(
                        dstT[:, f - f0, sl], z[:, 0, :], gg[:, f - f0:f - f0 + 1], rec[:, 0, :], ALU.mult, ALU.mult),
                        reads=[zkk, "gq", "rec"], writes=[(dk, tb)])
        for h in range(8):
            Wh, wk = self.wget(("mla_head", j, h))
            Wq = Wh[:, 0:768].rearrange("p (kc c) -> p kc c", kc=3, c=256)
            Wkv = Wh[:, 768:1280].rearrange("p (kc c) -> p kc c", kc=2, c=256)
            for tb in range(4):
                sl = slice(tb * 512, (tb + 1) * 512)
                pb, pk = self.bank(4, 8)
                for c in range(3):
                    k.op("pe", lambda c=c, pb=pb: nc.tensor.matmul(pb[:], Wq[:, c, 0:128], cqnT[:, c, sl], start=(c == 0), stop=(c == 2)),
                         reads=[wk, ("cqnT", tb)], writes=[pk])
                k.op("act", lambda pb=pb: nc.scalar.copy(QN[:, 0, sl], pb[:]), writes=[pk, ("QN", tb)])
                pb, pk = self.bank(4, 8)
                for c in range(3):
                    k.op("pe", lambda c=c, pb=pb: nc.tensor.matmul(pb[:], Wq[:, c, 128:256], cqnT[:, c, sl], start=(c == 0), stop=(c == 2)),
                         reads=[wk, ("cqnT", tb)], writes=[pk])
                k.op("dve", lambda pb=pb: nc.vector.tensor_tensor(QR[:, 0, sl], pb[:], self.CS[:, sl], ALU.mult), reads=["CS"], writes=[pk, ("QR", tb)])
                pb, pk = self.bank(4, 8)
                for c in range(2):
                    k.op("pe", lambda c=c, pb=pb: nc.tensor.matmul(pb[:], Wkv[:, c, 0:128], ckvnT[:, c, sl], start=(c == 0), stop=(c == 1)),
                         reads=[wk, ("ckvnT", tb)], writes=[pk])
                k.op("act", lambda pb=pb: nc.scalar.copy(KN[:, 0, sl], pb[:]), writes=[pk, ("KN", tb)])
            for g4 in range(4):
                pb, pk = self.bank(4, 8)
                for i in range(4):
                    kt = g4 * 4 + i
                    for c in range(2):
                        k.op("pe", lambda c=c, pb=pb, kt=kt, i=i: nc.tensor.matmul(
                            pb[:, i * 128:(i + 1) * 128], ckvnT[:, c, kt * 128:(kt + 1) * 128], Wkv[:, c, 128:256],
                            start=(c == 0), stop=(c == 1)), reads=[wk, ("ckvnT", g4)], writes=[pk])
                k.op("dve", lambda pb=pb, g4=g4: nc.vector.tensor_copy(
                    Vh[:, g4 * 4:(g4 + 1) * 4, :], pb[:].rearrange("p (a b) -> p a b", a=4, b=128)), writes=[pk, ("Vh", g4)])
            nb = 0
            for qb in range(4):
                nkt = 4 * qb + 4
                po, pok = self.ps[0], "ps0"
                pd, pdk = self.ps[1], "ps1"

                def score(kt):
                    q0 = max(qb * 512, kt * 128)
                    off = q0 - qb * 512
                    pb, pk = (self.ps[2], "ps2") if kt % 2 == 0 else (self.ps[3], "ps3")
                    k.op("pe", lambda: nc.tensor.matmul(pb[:, off:512], KN[:, 0, kt * 128:(kt + 1) * 128], QN[:, 0, q0:(qb + 1) * 512],
                                                        start=True, stop=False), reads=[("KN", kt // 4), ("QN", qb)], writes=[pk])
                    k.op("pe", lambda: nc.tensor.matmul(pb[:, off:512], KR2[:, 0, kt * 128:(kt + 1) * 128], QR[:, 0, q0:(qb + 1) * 512],
                                                        start=False, stop=True), reads=["KR2", ("QR", qb)], writes=[pk])
                    pt = PT[kt % 2]
                    ptk = f"PT{kt % 2}"
                    k.op("act", lambda: nc.scalar.activation(pt[:, 0, off:512], pb[:, off:512], AF.Exp, scale=scale), writes=[pk, ptk])
                    if kt >= 4 * qb:
                        k.op("dve", lambda: nc.vector.tensor_tensor(pt[:, 0, off:off + 128], pt[:, 0, off:off + 128], self.maskC[:], ALU.mult),
                             reads=["maskC"], writes=[ptk])
                    return pt, ptk, off

                def pv(kt, st):
                    pt, ptk, off = st
                    k.op("pe", lambda: nc.tensor.matmul(po[:, off:512], Vh[:, kt, :], pt[:, 0, off:512], start=(kt == 0), stop=(kt == nkt - 1)),
                         reads=[("Vh", kt // 4), ptk], writes=[pok])
                    k.op("pe", lambda: nc.tensor.matmul(pd[:, off:512], self.ones_b[:], pt[:, 0, off:512], start=(kt == 0), stop=(kt == nkt - 1)),
                         reads=["ones_b", ptk], writes=[pdk])
                prev = score(0)
                for kt in range(nkt):
                    nxt = score(kt + 1) if kt + 1 < nkt else None
                    pv(kt, prev)
                    prev = nxt
                k.op("act", lambda: nc.scalar.activation(rec[:, 0, :], pd[:], AF.Ln), writes=[pdk, "rec"])
                k.op("act", lambda: nc.scalar.activation(rec[:, 0, :], rec[:, 0, :], AF.Exp, scale=-1.0), writes=["rec"])
                k.op("dve", lambda: nc.vector.tensor_tensor(self.HT[:, h, qb * 512:(qb + 1) * 512], po[:], rec[:, 0, :], ALU.mult),
                     reads=["rec"], writes=[pok] + self.htk(qb))
        self.out_proj(self.HT, lambda tt: ("HT", tt), "mla_w_o", j)

    def emit_gdn(self, l, j):
        k, nc = self.k, self.nc
        self.norm_x(self.w["norm_mix"][l])
        k.barrier()
        off = [0]

        def A(shape, dtype, at=None):
            esz = 4 if dtype in (F32, I32) else 2
            n = (int(np.prod(shape)) * esz + 3) // 4 * 4
            if at is not None:
                return self.sv(at, shape, dtype)
            ap = self.sv(off[0], shape, dtype)
            off[0] += n
            return ap
        H = 8
        NCH = 32
        T = {n: A([NCH, H], F32)[0:64] for n in ("beta", "g", "gc", "kfac", "bg")}
        convw = A([96], F32)
        onb = A([128], F32)
        carry = A([3, 3], F32)
        qT = A([512], BF16)
        kT = A([512], BF16)
        vT = A([512], BF16)
        gT = A([512], BF16)
        qeT = A([512], BF16)
        kbg = A([8, 128], BF16)[0:64]
        kdec2 = [A([8, 128], BF16)[0:64] for _ in range(2)]
        vb = A([8, 128], BF16)[0:64]
        sg2 = [A([8, 128], BF16)[0:64] for _ in range(2)]
        o64 = A([8, 128], F32)[0:64]
        mat_off = off[0]
        mt = {n: A([8, 64], F32)[0:64] for n in ("T1", "Dec", "DecT", "M", "Mt", "A1", "At1", "Pt")}
        for i, n in enumerate(("dtb", "alog", "ta", "tb")):
            T[n] = A([NCH, H], F32, at=mat_off + i * 1024)[0:64]
        Ttb = A([8, 64], BF16)[0:64]
        attnT = A([8, 64], BF16)[0:64]
        wTb = A([8, 64], BF16)
        u_off = off[0]
        u_sb = A([8, 128], F32)[0:64]
        zin = [A([515], F32), A([515], F32), A([515], F32, at=mat_off)]
        zink = ["zin0", "zin1", "zin2"]
        acc = [A([512], F32) for _ in range(3)]
        og = acc[1].bitcast(BF16)[0:64, :].rearrange("p (a b) -> p a b", a=8, b=128)
        ogT = acc[2].bitcast(BF16)[:, 0:512]
        sqo = acc[0].bitcast(BF16)[0:64, :].rearrange("p (a b) -> p a b", a=8, b=128)
        sq = [A([512], BF16) for _ in range(2)]
        rs = [A([512], F32) for _ in range(2)]
        vnb = A([128], BF16)[0:64]
        Sf = A([128], F32)
        Sb = A([128], BF16)
        egl8 = A([8], F32)
        ssg = A([32], F32)[0:64]
        ones_f = self.stage2
        ltri64 = self.ltri[0:64, 0:64]
        idf = self.ident_f[0:64, 0:64]
        idb64 = self.ident_b[0:64, 0:64]
        B3 = [64, 8, 64]

        def bc(ap2d):
            return ap2d.unsqueeze(1).to_broadcast(B3)

        def dram_bcast(ap1d, n0, n1):
            return bass.AP(ap1d.tensor, ap1d.offset, [[0, 64], [0, n1], [1, n0]])
        k.dma("sp", T["dtb"], dram_bcast(self.w["gdn_dt_bias"][j], H, NCH), writes=["dtb"])
        k.dma("sp", T["alog"], dram_bcast(self.w["gdn_a_log"][j], H, NCH), writes=["alog"])
        k.dma("sp", onb, self.w["gdn_o_norm"][j].partition_broadcast(128), writes=["onb"])
        k.op("dve", lambda: nc.vector.memset(ones_f[:], 1.0), writes=["ones_f"])
        self.load_cols(convw, self.w["gdn_conv_w"][j].rearrange("j (c p) -> (j c) p", p=128), 96, "convw")
        Wt, wk = self.wget(("cols", "gdn_w_in", j, 4096, 16))
        Wv = Wt[:, 0:128].rearrange("p (kc c) -> p kc c", kc=8, c=16)
        pb, pk = self.bank()
        for c in range(NCH):
            for kc in range(KC):
                k.op("pe", lambda c=c, kc=kc: nc.tensor.matmul(pb[0:64, c * 16:(c + 1) * 16], self.HT[:, kc, c * 64:(c + 1) * 64], Wv[:, kc, :],
                                                                start=(kc == 0), stop=(kc == KC - 1)),
                     reads=[wk, ("HT", c // 2)], writes=[pk])
        pbv = pb[0:64, :].rearrange("p (c f) -> p c f", c=NCH, f=16)
        k.op("act", lambda: nc.scalar.activation(T["ta"], pbv[:, :, 0:8], AF.Exp, scale=-1.0), writes=[pk, "ta"])
        k.op("dve", lambda: nc.vector.tensor_scalar(T["ta"], T["ta"], 1.0, None, ALU.add), writes=["ta"])
        k.op("dve", lambda: nc.vector.reciprocal(T["beta"], T["ta"]), reads=["ta"], writes=["beta"])
        k.op("dve", lambda: nc.vector.tensor_tensor(T["tb"], pbv[:, :, 8:16], T["dtb"], ALU.add), reads=["dtb"], writes=[pk, "tb"])
        k.op("act", lambda: nc.scalar.activation(T["tb"], T["tb"], AF.Exp), writes=["tb"])
        k.op("act", lambda: nc.scalar.activation(T["tb"], T["tb"], AF.Ln, bias=1.0), writes=["tb"])
        k.op("act", lambda: nc.scalar.activation(T["alog"], T["alog"], AF.Exp), writes=["alog"])
        k.op("dve", lambda: nc.vector.scalar_tensor_tensor(T["g"], T["tb"], -1.0, T["alog"], ALU.mult, ALU.mult), reads=["tb", "alog"], writes=["g"])
        pb, pk = self.bank()
        pb2, pk2 = self.bank()
        for c in range(NCH):
            k.op("pe", lambda c=c: nc.tensor.matmul(pb[0:64, c * 8:(c + 1) * 8], ltri64, T["g"][:, c, :], start=True, stop=True),
                 reads=["g", "ltri"], writes=[pk])
            k.op("pe", lambda c=c: nc.tensor.matmul(pb2[0:64, c * 8:(c + 1) * 8], ones_f[0:64, 0:64], T["g"][:, c, :], start=True, stop=True),
                 reads=["g", "ones_f"], writes=[pk2])
        k.op("dve", lambda: nc.vector.tensor_copy(T["gc"], pb[0:64, 0:256].rearrange("p (c f) -> p c f", c=NCH, f=8)), writes=[pk, "gc"])
        k.op("dve", lambda: nc.vector.tensor_tensor(T["ta"], pb2[0:64, 0:256].rearrange("p (c f) -> p c f", c=NCH, f=8), T["gc"], ALU.subtract),
             reads=["gc"], writes=[pk2, "ta"])
        k.op("act", lambda: nc.scalar.activation(T["kfac"], T["ta"], AF.Exp), reads=["ta"], writes=["kfac"])
        k.op("act", lambda: nc.scalar.activation(T["tb"], T["gc"], AF.Exp), reads=["gc"], writes=["tb"])
        k.op("dve", lambda: nc.vector.tensor_tensor(T["bg"], T["beta"], T["tb"], ALU.mult), reads=["beta", "tb"], writes=["bg"])
        k.barrier()
        n_alt = [0]

        def cp(dst, src, reads=(), writes=()):
            n_alt[0] += 1
            if n_alt[0] % 2 == 0:
                k.op("act", lambda: nc.scalar.copy(dst, src), reads=reads, writes=writes)
            else:
                k.op("dve", lambda: nc.vector.tensor_copy(dst, src), reads=reads, writes=writes)

        def v3(pbank, parts=64):
            return pbank[0:parts, :].rearrange("p (c j) -> p c j", c=8, j=64)

        units = [(h, seg) for h in range(H) for seg in range(4)]
        B4 = [64, 8, 128]
        wts = {}

        def get_w(h):
            if h not in wts:
                Wh, wkh = self.wget(("multi", "gdn_w_in", j, ((h * 128, 128), (1024 + h * 128, 128), (2048 + h * 128, 128), (3072 + h * 128, 128))))
                wts[h] = (Wh[:, 0:4096].rearrange("p (kc c) -> p kc c", kc=8, c=512), wkh)
            return wts[h]

        def get_wo(h):
            if ("o", h) not in wts:
                wts[("o", h)] = self.wget(("rows", "gdn_w_o", j, h * 128, 128))
            return wts[("o", h)]

        def P(u):
            h, seg = units[u]
            Whv, wkh = get_w(h)
            t0 = seg * 512
            c0 = seg * 8
            hts = self.htk(seg)
            if seg == 0:
                k.op("dve", lambda: nc.vector.memset(carry, 0.0), writes=["carry"])
            banks = {}
            for part in (0, 1, 2, 3):
                pb, pk = (self.ps[0], "ps0") if part == 3 else (self.ps[1 + part], f"ps{1 + part}")
                banks[part] = (pb, pk)
                for kc in range(KC):
                    k.op("pe", lambda kc=kc: nc.tensor.matmul(pb[:], Whv[:, kc, part * 128:(part + 1) * 128], self.HT[:, kc, t0:t0 + 512],
                                                              start=(kc == 0), stop=(kc == KC - 1)), reads=[wkh] + hts, writes=[pk])
                    if kc % 4 == 3 and part < 3:
                        yield
            for part in range(3):
                pb, pk = banks[part]
                k.op("act", lambda: nc.scalar.copy(zin[part][:, 3:515], pb[:]), writes=[pk, zink[part]] + ([("T1", 0), ("T1", 1), ("Dec", 0), ("Dec", 1)] if part == 2 else []))
                yield
            for part in range(3):
                z, zk, ac, ak = zin[part], zink[part], acc[part], f"acc{part}"
                zal = [("T1", 0), ("T1", 1), ("Dec", 0), ("Dec", 1)] if part == 2 else []
                k.op("dve", lambda: nc.vector.tensor_copy(z[:, 0:3], carry[:, part, :]), reads=["carry"], writes=[zk] + zal)
                k.op("dve", lambda: nc.vector.tensor_copy(carry[:, part, :], z[:, 512:515]), reads=[zk], writes=["carry"])
                for jj in (3, 2, 1, 0):
                    wc = convw[:, jj * 24 + part * 8 + h:jj * 24 + part * 8 + h + 1]
                    if jj == 3:
                        k.op("dve", lambda: nc.vector.tensor_scalar(ac, z[:, 3:515], wc, None, ALU.mult), reads=["convw", zk], writes=[ak])
                        yield
                    else:
                        k.op("dve", lambda: nc.vector.scalar_tensor_tensor(ac, z[:, jj:jj + 512], wc, ac, ALU.mult, ALU.add),
                             reads=["convw", zk], writes=[ak])
                        yield
            pb, pk = banks[3]
            k.op("act", lambda: nc.scalar.activation(gT, pb[:], AF.Silu), writes=[pk, "gT"])
            k.op("act", lambda: nc.scalar.activation(acc[0], acc[0], AF.Silu), writes=["acc0"])
            yield
            k.op("act", lambda: nc.scalar.activation(acc[1], acc[1], AF.Silu), writes=["acc1"])
            k.op("act", lambda: nc.scalar.activation(vT, acc[2], AF.Silu), reads=["acc2"], writes=["vT"])
            for part in range(2):
                k.op("act", lambda: nc.scalar.activation(sq[part], acc[part], AF.Square), reads=[f"acc{part}"], writes=[f"sq{part}"])
            yield
            pbs = {}
            for part in range(2):
                pb, pk = self.bank()
                pbs[part] = (pb, pk)
                k.op("pe", lambda: nc.tensor.matmul(pb[:], self.ones_b[:], sq[part], start=True, stop=True), reads=["ones_b", f"sq{part}"], writes=[pk])
            for part in range(2):
                pb, pk = pbs[part]
                k.op("act", lambda: nc.scalar.activation(rs[part], pb[:], AF.Ln, bias=EPS), writes=[pk, f"rs{part}"])
            for part in range(2):
                k.op("act", lambda: nc.scalar.activation(rs[part], rs[part], AF.Exp, scale=-0.5), writes=[f"rs{part}"])
            yield
            k.op("dve", lambda: nc.vector.scalar_tensor_tensor(qT, acc[0], 128 ** -0.5, rs[0], ALU.mult, ALU.mult), reads=["acc0", "rs0"], writes=["qT"])
            yield
            k.op("dve", lambda: nc.vector.scalar_tensor_tensor(kT, acc[1], 1.0, rs[1], ALU.mult, ALU.mult), reads=["acc1", "rs1"], writes=["kT"])
            yield
            B4 = [64, 8, 128]
            for name, srcT in (("k", kT), ("v", vT), ("g", gT)):
                pb, pk = self.bank()
                pv = pb[:].bitcast(BF16).rearrange("p (a b) -> p a b", a=8, b=128)
                for cc in range(8):
                    k.op("pe", lambda cc=cc: nc.tensor.transpose(pv[0:64, cc, :], srcT[:, cc * 64:(cc + 1) * 64], self.ident_b[:]),
                         reads=[name + "T", "ident_b"], writes=[pk])
                if name == "g":
                    k.op("dve", lambda: nc.vector.tensor_tensor(sg2[u % 2], pv[0:64], onb[0:64, :].unsqueeze(1).to_broadcast(B4), ALU.mult),
                         reads=["onb"], writes=[pk, f"sg{u % 2}"])
                elif name == "k":
                    k.op("dve", lambda: nc.vector.tensor_tensor(kbg, pv[0:64], T["bg"][:, c0:c0 + 8, h:h + 1].to_broadcast(B4), ALU.mult),
                         reads=["bg"], writes=[pk, "kbg"])
                    k.op("dve", lambda: nc.vector.tensor_tensor(kdec2[u % 2], pv[0:64], T["kfac"][:, c0:c0 + 8, h:h + 1].to_broadcast(B4), ALU.mult),
                         reads=["kfac"], writes=[pk, f"kdec{u % 2}"])
                else:
                    k.op("dve", lambda: nc.vector.tensor_tensor(vb, pv[0:64], T["beta"][:, c0:c0 + 8, h:h + 1].to_broadcast(B4), ALU.mult),
                         reads=["beta"], writes=[pk, "vb"])
                yield

        def N(u):
            h, seg = units[u]
            t0 = seg * 512
            c0 = seg * 8
            hts = self.htk(seg)
            G4 = [64, 4, 64]

            def bc4(ap2d):
                return ap2d.unsqueeze(1).to_broadcast(G4)

            def v4(pbank, parts=64):
                return pbank[0:parts, 0:256].rearrange("p (c j) -> p c j", c=4, j=64)
            gs_ = [slice(4 * g, 4 * g + 4) for g in range(2)]
            for g in range(2):
                gl = gs_[g]
                cg = slice(c0 + 4 * g, c0 + 4 * g + 4)
                tsl = slice(256 * g, 256 * g + 256)
                gsl = T["g"][:, cg, h:h + 1].to_broadcast(G4)
                gcb = T["gc"][:, cg, h:h + 1].to_broadcast(G4)
                btb = T["beta"][:, cg, h:h + 1].to_broadcast(G4)
                k.op("dve", lambda: nc.vector.tensor_tensor(mt["T1"][:, gl, :], bc4(ltri64), gsl, ALU.mult), reads=["g", "ltri"], writes=[("T1", g)])
                pg, pgk = self.bank()
                for i in range(4):
                    k.op("pe", lambda i=i: nc.tensor.matmul(pg[:, i * 64:(i + 1) * 64], ones_f[0:64, :], mt["T1"][:, 4 * g + i, :], start=True, stop=True),
                         reads=["ones_f", ("T1", g)], writes=[pgk])
                pg3 = pg[:, 0:256].rearrange("p (c j) -> p c j", c=4, j=64)
                k.op("act", lambda: nc.scalar.activation(egl8[:, gl], pg3[:, :, 63], AF.Exp), writes=[pgk, ("egl8", g)])
                k.op("act", lambda: nc.scalar.activation(rs[0][:, tsl], pg[:, 0:256], AF.Exp), writes=[pgk, "rs0"])
                k.op("dve", lambda: nc.vector.tensor_tensor(qeT[:, tsl], qT[:, tsl], rs[0][:, tsl], ALU.mult), reads=["qT", "rs0"], writes=[("qeT", g)])
                k.op("dve", lambda: nc.vector.tensor_tensor(mt["T1"][:, gl, :], v4(pg), gcb, ALU.subtract), reads=["gc"], writes=[pgk, ("T1", g)])
                k.op("dve", lambda: nc.vector.tensor_tensor(mt["Dec"][:, gl, :], mt["T1"][:, gl, :], bc4(self.mposL[0:64, 0:64]), ALU.add),
                     reads=[("T1", g), "mposL"], writes=[("Dec", g)])
                k.op("act", lambda: nc.scalar.activation(mt["Dec"][:, gl, :], mt["Dec"][:, gl, :], AF.Exp, scale=-1.0), writes=[("Dec", g)])
                k.op("dve", lambda: nc.vector.tensor_tensor(mt["DecT"][:, gl, :], mt["T1"][:, gl, :], bc4(self.mposU[0:64, 0:64]), ALU.subtract),
                     reads=[("T1", g), "mposU"], writes=[("DecT", g)])
                k.op("act", lambda: nc.scalar.activation(mt["DecT"][:, gl, :], mt["DecT"][:, gl, :], AF.Exp), writes=[("DecT", g)])
                k.op("dve", lambda: nc.vector.tensor_tensor(mt["Dec"][:, gl, :], mt["Dec"][:, gl, :], bc4(self.strict[0:64, 0:64]), ALU.mult),
                     reads=["strict"], writes=[("Dec", g)])
                k.op("dve", lambda: nc.vector.tensor_tensor(mt["Dec"][:, gl, :], mt["Dec"][:, gl, :], btb, ALU.mult), reads=["beta"], writes=[("Dec", g)])
                yield
            for g in range(2):
                gl = gs_[g]
                pkk, pkkk = self.bank()
                for i in range(4):
                    cs = slice((4 * g + i) * 64, (4 * g + i + 1) * 64)
                    k.op("pe", lambda i=i, cs=cs: nc.tensor.matmul(pkk[0:64, i * 64:(i + 1) * 64], kT[:, cs], kT[:, cs], start=True, stop=True),
                         reads=["kT"], writes=[pkkk])
                k.op("dve", lambda: nc.vector.tensor_tensor(mt["M"][:, gl, :], v4(pkk), mt["Dec"][:, gl, :], ALU.mult), reads=[("Dec", g)], writes=[pkkk, ("M", g)])
                yield
            for g in range(2):
                gl = gs_[g]
                pt, ptk = self.bank()
                for i in range(4):
                    k.op("pe", lambda i=i: nc.tensor.transpose(pt[0:64, i * 64:(i + 1) * 64], mt["M"][:, 4 * g + i, :], idf), reads=[("M", g), "ident_f"], writes=[ptk])
                k.op("act", lambda: nc.scalar.copy(mt["Mt"][:, gl, :], v4(pt)), writes=[ptk, ("Mt", g)])
                k.op("dve", lambda: nc.vector.tensor_tensor(mt["Pt"][:, gl, :], bc4(idf), v4(pt), ALU.subtract), reads=["ident_f"], writes=[ptk, ("Pt", g)])
                yield
            cur = ("M", "Mt")
            for s_ in range(1, 6):
                nxt = ("A1", "At1") if cur[0] == "M" else ("M", "Mt")
                Ac, Atc = mt[cur[0]], mt[cur[1]]
                An, Atn = mt[nxt[0]], mt[nxt[1]]
                for g in range(2):
                    gl = gs_[g]
                    pa, pak = self.bank()
                    for i in range(4):
                        c_ = 4 * g + i
                        k.op("pe", lambda i=i, c_=c_: nc.tensor.matmul(pa[0:64, i * 64:(i + 1) * 64], Atc[:, c_, :], Ac[:, c_, :], start=True, stop=True),
                             reads=[(cur[0], g), (cur[1], g)], writes=[pak])
                    k.op("act", lambda: nc.scalar.copy(An[:, gl, :], v4(pa)), writes=[pak, (nxt[0], g)])
                    if s_ < 5:
                        pat, patk = self.bank()
                        for i in range(4):
                            c_ = 4 * g + i
                            k.op("pe", lambda i=i, c_=c_: nc.tensor.matmul(pat[0:64, i * 64:(i + 1) * 64], Ac[:, c_, :], Atc[:, c_, :], start=True, stop=True),
                                 reads=[(cur[0], g), (cur[1], g)], writes=[patk])
                        k.op("dve", lambda: nc.vector.tensor_copy(Atn[:, gl, :], v4(pat)), writes=[patk, (nxt[1], g)])
                    yield
                for g in range(2):
                    gl = gs_[g]
                    pp, ppk = self.bank()
                    for i in range(4):
                        c_ = 4 * g + i
                        k.op("pe", lambda i=i, c_=c_: nc.tensor.matmul(pp[0:64, i * 64:(i + 1) * 64], An[:, c_, :], mt["Pt"][:, c_, :], start=True, stop=True),
                             reads=[(nxt[0], g), ("Pt", g)], writes=[ppk])
                    if s_ < 5:
                        k.op("dve", lambda: nc.vector.tensor_tensor(mt["Pt"][:, gl, :], v4(pp), mt["Pt"][:, gl, :], ALU.add), writes=[ppk, ("Pt", g)])
                    else:
                        k.op("dve", lambda: nc.vector.tensor_tensor(Ttb[:, gl, :], v4(pp), mt["Pt"][:, gl, :], ALU.add), reads=[("Pt", g)], writes=[ppk, ("Ttb", g)])
                    yield
                cur = nxt
            for g in range(2):
                gl = gs_[g]
                pq, pqk = self.bank()
                for i in range(4):
                    cs = slice((4 * g + i) * 64, (4 * g + i + 1) * 64)
                    k.op("pe", lambda i=i, cs=cs: nc.tensor.matmul(pq[0:64, i * 64:(i + 1) * 64], kT[:, cs], qT[:, cs], start=True, stop=True),
                         reads=["kT", "qT"], writes=[pqk])
                k.op("dve", lambda: nc.vector.tensor_tensor(attnT[:, gl, :], v4(pq), mt["DecT"][:, gl, :], ALU.mult), reads=[("DecT", g)], writes=[pqk, ("attnT", g)])
                yield
            for g in range(2):
                gl = gs_[g]
                pu, puk = self.bank()
                for i in range(4):
                    c_ = 4 * g + i
                    k.op("pe", lambda c_=c_, i=i: nc.tensor.matmul(pu[0:64, i * 128:(i + 1) * 128], Ttb[:, c_, :], vb[:, c_, :], start=True, stop=True),
                         reads=[("Ttb", g), "vb"], writes=[puk])
                k.op("act", lambda: nc.scalar.copy(u_sb[:, gl, :], pu[0:64, :].rearrange("p (a b) -> p a b", a=4, b=128)),
                     writes=[puk, ("u", g)])
                pw, pwk = self.bank()
                for i in range(4):
                    c_ = 4 * g + i
                    k.op("pe", lambda c_=c_, i=i: nc.tensor.matmul(pw[:, i * 64:(i + 1) * 64], kbg[:, c_, :], Ttb[:, c_, :], start=True, stop=True),
                         reads=[("Ttb", g), "kbg"], writes=[pwk])
                k.op("dve", lambda: nc.vector.tensor_copy(wTb[:, gl, :], pw[:, 0:256].rearrange("p (c j) -> p c j", c=4, j=64)), writes=[pwk, ("wTb", g)])
                yield
        def C(u):
            h, seg = units[u]
            t0 = seg * 512
            c0 = seg * 8
            hts = self.htk(seg)
            if seg == 0:
                k.op("dve", lambda: nc.vector.memset(Sf, 0.0), writes=["Sf"])
                k.op("dve", lambda: nc.vector.memset(Sb, 0.0), writes=["Sb"])
            for cc in range(8):
                cs = slice(cc * 64, (cc + 1) * 64)
                p1, p1k = self.bank()
                g = cc // 4
                k.op("pe", lambda: nc.tensor.matmul(p1[0:64, 0:128], wTb[:, cc, :], Sb, start=True, stop=True), reads=[("wTb", g), "Sb"], writes=[p1k])
                k.op("dve", lambda: nc.vector.tensor_tensor(vnb, u_sb[:, cc, :], p1[0:64, 0:128], ALU.subtract), reads=[("u", g)], writes=[p1k, "vnb"])
                yield
                p4, p4k = self.bank()
                k.op("pe", lambda: nc.tensor.matmul(p4[:, 0:128], kdec2[u % 2][:, cc, :], vnb, start=True, stop=True), reads=[f"kdec{u % 2}", "vnb"], writes=[p4k])
                p3, p3k = self.bank()
                k.op("pe", lambda: nc.tensor.matmul(p3[0:64, 0:128], qeT[:, cs], Sb, start=True, stop=False), reads=[("qeT", g), "Sb"], writes=[p3k])
                k.op("pe", lambda: nc.tensor.matmul(p3[0:64, 0:128], attnT[:, cc, :], vnb, start=False, stop=True), reads=[("attnT", g), "vnb"], writes=[p3k])
                k.op("dve", lambda: nc.vector.scalar_tensor_tensor(Sb, Sf, egl8[:, cc:cc + 1], p4[:, 0:128], ALU.mult, ALU.add),
                     reads=[("egl8", g), "Sf"], writes=[p4k, "Sb"])
                k.op("dve", lambda: nc.vector.scalar_tensor_tensor(Sf, Sf, egl8[:, cc:cc + 1], p4[:, 0:128], ALU.mult, ALU.add),
                     reads=[("egl8", g)], writes=[p4k, "Sf"])
                k.op("act", lambda: nc.scalar.copy(o64[:, cc, :], p3[0:64, 0:128]), writes=[p3k, "o64"])
                yield

        def O(u):
            h, seg = units[u]
            Wo, wko = get_wo(h)
            t0 = seg * 512
            c0 = seg * 8
            hts = self.htk(seg)
            k.op("act", lambda: nc.scalar.activation(sqo, o64, AF.Square), reads=["o64"], writes=["acc0"])
            k.op("dve", lambda: nc.vector.tensor_reduce(ssg[:, 0:8], sqo, mybir.AxisListType.X, ALU.add), reads=["acc0"], writes=["ssg"])
            k.op("act", lambda: nc.scalar.activation(ssg[:, 8:16], ssg[:, 0:8], AF.Ln, scale=1.0 / 128, bias=EPS), writes=["ssg"])
            k.op("act", lambda: nc.scalar.activation(ssg[:, 16:24], ssg[:, 8:16], AF.Exp, scale=-0.5), writes=["ssg"])
            yield
            k.op("dve", lambda: nc.vector.tensor_tensor(o64, o64, ssg[:, 16:24].unsqueeze(2).to_broadcast(B4), ALU.mult), reads=["ssg"], writes=["o64"])
            k.op("dve", lambda: nc.vector.tensor_tensor(og, o64, sg2[u % 2], ALU.mult), reads=["o64", f"sg{u % 2}"], writes=["acc1"])
            yield
            pb, pk = self.bank()
            pv = pb[:].bitcast(BF16).rearrange("p (a b) -> p a b", a=16, b=64)
            for cc in range(8):
                k.op("pe", lambda cc=cc: nc.tensor.transpose(pv[:, cc, :], og[:, cc, :], idb64), reads=["acc1", "ident_b"], writes=[pk])
            cp(ogT.rearrange("p (a b) -> p a b", a=8, b=64), pv[:, 0:8, :], writes=[pk, "acc2"])
            Wov = Wo[:, 0:1024]
            for tt in range(4):
                for db in range(2):
                    pb, pk = self.bank()
                    k.op("pe", lambda: nc.tensor.matmul(pb[:], ogT[:, tt * 128:(tt + 1) * 128], Wov[:, db * 512:(db + 1) * 512], start=True, stop=True),
                         reads=["acc2", wko], writes=[pk])
                    self.add_to_x(pb, pk, seg * 4 + tt, db * 512)
                yield

        def run(*gens):
            gens = list(gens)
            while gens:
                for g_ in list(gens):
                    try:
                        next(g_)
                    except StopIteration:
                        gens.remove(g_)

        self.bank_lo = 4
        run(P(0))
        run(N(0))
        def seq(*gs):
            for g_ in gs:
                yield from g_

        for u in range(len(units)):
            if u + 1 < len(units):
                run(C(u), P(u + 1))
                run(O(u), N(u + 1))
            else:
                run(seq(C(u), O(u)))
        self.bank_lo = 0
        print("gdn scratch bytes", off[0])

    def emit_final(self, do_norm=True):
        k, nc = self.k, self.nc
        k.barrier()
        ob = [self.sv(i * 4096, [1, D], F32) for i in range(2)]
        ov = self.out_d.rearrange("(t p) d -> p t d", p=128)
        ss = self.ss
        if do_norm:
            k.dma("sp", self.Gb[:], self.w["final_norm"].partition_broadcast(128), writes=["Gb"])
            k.op("dve", lambda: nc.vector.memset(ss[:, 0:16], 0.0), writes=["ss"])
            for t in range(NT):
                k.op("act", lambda t=t: nc.scalar.activation(self.junk[:], self.X[:, t, :], AF.Square, accum_out=ss[:, t:t + 1]),
                     reads=[("X", t)], writes=["junk", "ss"])
            k.op("act", lambda: nc.scalar.activation(ss[:, 16:32], ss[:, 0:16], AF.Ln, scale=1.0 / D, bias=EPS), writes=["ss"])
            k.op("act", lambda: nc.scalar.activation(ss[:, 32:48], ss[:, 16:32], AF.Exp, scale=-0.5), writes=["ss"])
        for t in range(NT):
            if do_norm:
                o = ob[t % 2]
                okk = f"ob{t % 2}"
                k.op("dve", lambda t=t, o=o: nc.vector.scalar_tensor_tensor(o[:, 0, :], self.X[:, t, :], ss[:, 32 + t:33 + t], self.Gb[:],
                                                                           ALU.mult, ALU.mult), reads=[("X", t), "ss", "Gb"], writes=[okk])
                k.dma("sp", ov[:, t, :], o[:, 0, :], reads=[okk], sname=f"out{t % 2}")
            else:
                k.dma("sp", ov[:, t, :], self.X[:, t, :], reads=[("X", t)], sname="out")


FULL_PLAN = [("mla", 0, 0), ("xattn", 0), ("mlp", 0),
             ("gdn", 1, 0), ("xattn", 1), ("mlp", 1),
             ("sc", 2, 0), ("xattn", 2), ("mlp", 2),
             ("mla", 3, 1), ("xattn", 3), ("mlp", 3),
             ("final",)]


def run_plan(plan, inputs, cores):
    prog = Prog(plan)
    nc = prog.build()
    cst, _ = host_consts()
    in_maps = []
    for b in cores:
        m = {"x": np.ascontiguousarray(inputs["x"][b]), "mem": np.ascontiguousarray(inputs["mem"][b]),
             "pos": np.ascontiguousarray(inputs["positions"][b]).astype(np.int32), "cst": cst}
        for n in W_NAMES:
            m[n] = np.ascontiguousarray(np.asarray(inputs[n], dtype=np.float32))
        in_maps.append(m)
    res = run_bass_kernel_spmd(nc, in_maps, core_ids=list(range(len(cores))))
    return np.stack([r["out"] for r in res.results], axis=0)


def kernel(**inputs):
    inputs = {k: np.asarray(v) for k, v in inputs.items()}
    out = run_plan(FULL_PLAN, inputs, list(range(8)))
    return out.astype(np.float32)
```
